# Optimizing a Trainium2 kernel written in Bass

```python
import jax, jax.numpy as jnp
from jax import lax
import numpy as np

D_MODEL = 2048
BATCH = 4
SEQ = 2048
DEPTH = 1
DEC_BATCH = 128
DEC_SEQ = 8
PAST_LEN = 16384
PAGE_SIZE = 128

D_MIX = D_MODEL
LRU_WIDTH = D_MIX // 2
LRU_HEADS = 8
LRU_HEAD_DIM = LRU_WIDTH // LRU_HEADS
LRU_C = 8.0
CONV_WIDTH = 4
SGU_WIDTH = D_MIX - LRU_WIDTH
SGU_HEADS = 8
SGU_HEAD_DIM = SGU_WIDTH // SGU_HEADS
CHUNK = 128
PROJ_WIDTH = 2 * LRU_WIDTH + 3 * SGU_WIDTH
EPS = 1e-6

kernel_name = "hymba_rglru_chunk_sgu_decode_step"


def rms_norm(x, g):
    xf = x.astype(jnp.float32)
    y = xf * lax.rsqrt(jnp.mean(xf * xf, axis=-1, keepdims=True) + EPS)
    return (y * g.astype(jnp.float32)).astype(x.dtype)


def causal_depthwise_conv(xb, buf, w, b):
    T = xb.shape[1]
    xp = jnp.concatenate([buf.astype(xb.dtype), xb], axis=1)
    y = b + xp[:, 0:T] * w[0]
    for k in range(1, CONV_WIDTH):
        y = y + xp[:, k:k + T] * w[k]
    return y, xp[:, -(CONV_WIDTH - 1):]


def rg_lru(x, reset, h0, w_r, b_r, w_i, b_i, lam):
    N, T, W = x.shape
    xh = x.reshape(N, T, LRU_HEADS, LRU_HEAD_DIM)
    r = jax.nn.sigmoid(jnp.einsum('nthi,hij->nthj', xh, w_r) + b_r).reshape(N, T, W).astype(jnp.float32)
    i = jax.nn.sigmoid(jnp.einsum('nthi,hij->nthj', xh, w_i) + b_i).reshape(N, T, W).astype(jnp.float32)
    log_a = -LRU_C * r * jax.nn.softplus(-lam.astype(jnp.float32))
    a = jnp.exp(log_a)
    mult = jnp.sqrt(-jnp.expm1(2.0 * log_a))
    rs = reset[None, :, None]
    mult = jnp.where(rs, 1.0, mult)
    a = jnp.where(rs, 0.0, a)
    bterm = mult * i * x.astype(jnp.float32)
    bterm = bterm.at[:, 0].add(a[:, 0] * h0.astype(jnp.float32))

    def combine(c1, c2):
        a1, b1 = c1
        a2, b2 = c2
        return a1 * a2, a2 * b1 + b2

    _, h = lax.associative_scan(combine, (a, bterm), axis=1)
    return h.astype(x.dtype), h[:, -1].astype(h0.dtype)


def chunk_spatial_gating(u, v, w_s, b_s):
    N, T, W = u.shape
    L = min(T, CHUNK)
    n_chunks = T // L
    mask = jnp.tril(jnp.ones((L, L), dtype=bool))
    ws = jnp.where(mask[None], w_s[:, :L, :L], 0.0).astype(v.dtype)
    vc = v.reshape(N, n_chunks, L, SGU_HEADS, SGU_HEAD_DIM)
    s = jnp.einsum('hts,ncshd->ncthd', ws, vc) + b_s[:, :L].T[None, None, :, :, None]
    return u * s.reshape(N, T, W)


def mixer_layer(x, pos0, conv_buf, h0, pre_g, post_g, w_in, conv_w, conv_b,
                w_r, b_r, w_i, b_i, lam, sgu_g, w_s, b_s, w_out):
    T = x.shape[1]
    z = rms_norm(x, pre_g)
    proj = z @ w_in
    o1 = LRU_WIDTH
    o2 = o1 + LRU_WIDTH
    o3 = o2 + SGU_WIDTH
    o4 = o3 + SGU_WIDTH
    xr, gr, u, v, gs = (proj[..., :o1], proj[..., o1:o2], proj[..., o2:o3],
                        proj[..., o3:o4], proj[..., o4:])
    xc, new_buf = causal_depthwise_conv(xr, conv_buf, conv_w, conv_b)
    reset = (pos0 + jnp.arange(T)) == 0
    hr, h_last = rg_lru(xc, reset, h0, w_r, b_r, w_i, b_i, lam)
    br = hr * jax.nn.silu(gr)
    u = jax.nn.gelu(u)
    v = rms_norm(jax.nn.gelu(v), sgu_g)
    bs = chunk_spatial_gating(u, v, w_s, b_s) * jax.nn.silu(gs)
    out = jnp.concatenate([br, bs], axis=-1) @ w_out
    y = x + rms_norm(out, post_g)
    return y, new_buf, h_last, v


def setup_inputs(seed: int = 0) -> dict:
    key = jax.random.key(seed)
    ks = jax.random.split(key, 20)
    f32 = jnp.float32
    a_c = jax.random.uniform(ks[9], (DEPTH, LRU_WIDTH), f32, 0.9, 0.999)
    a = a_c ** (1.0 / LRU_C)
    lam = jnp.log(a) - jnp.log1p(-a)
    return {
        "x_prompt": jax.random.normal(ks[0], (BATCH, SEQ, D_MODEL), f32),
        "x_sample": jax.random.normal(ks[1], (DEC_BATCH, DEC_SEQ, D_MODEL), f32),
        "state_rglru_conv": jax.random.normal(ks[2], (DEPTH, DEC_BATCH, CONV_WIDTH - 1, LRU_WIDTH), f32),
        "state_rglru_h": 0.5 * jax.random.normal(ks[3], (DEPTH, DEC_BATCH, LRU_WIDTH), f32),
        "pre_norm_g": 1.0 + 0.02 * jax.random.normal(ks[4], (DEPTH, D_MODEL), f32),
        "post_norm_g": 1.0 + 0.02 * jax.random.normal(ks[5], (DEPTH, D_MODEL), f32),
        "w_in": jax.random.normal(ks[6], (DEPTH, D_MODEL, PROJ_WIDTH), f32) * D_MODEL ** -0.5,
        "conv_w": jax.random.normal(ks[7], (DEPTH, CONV_WIDTH, LRU_WIDTH), f32) * CONV_WIDTH ** -0.5,
        "conv_b": 0.01 * jax.random.normal(ks[8], (DEPTH, LRU_WIDTH), f32),
        "w_rgate": jax.random.normal(ks[10], (DEPTH, LRU_HEADS, LRU_HEAD_DIM, LRU_HEAD_DIM), f32) * LRU_HEAD_DIM ** -0.5,
        "b_rgate": 0.01 * jax.random.normal(ks[11], (DEPTH, LRU_HEADS, LRU_HEAD_DIM), f32),
        "w_igate": jax.random.normal(ks[12], (DEPTH, LRU_HEADS, LRU_HEAD_DIM, LRU_HEAD_DIM), f32) * LRU_HEAD_DIM ** -0.5,
        "b_igate": 0.01 * jax.random.normal(ks[13], (DEPTH, LRU_HEADS, LRU_HEAD_DIM), f32),
        "lru_lambda": lam,
        "sgu_norm_g": 1.0 + 0.02 * jax.random.normal(ks[14], (DEPTH, SGU_WIDTH), f32),
        "w_spatial": jax.random.normal(ks[15], (DEPTH, SGU_HEADS, CHUNK, CHUNK), f32) * CHUNK ** -0.5,
        "b_spatial": 1.0 + 0.02 * jax.random.normal(ks[16], (DEPTH, SGU_HEADS, CHUNK), f32),
        "w_out": jax.random.normal(ks[17], (DEPTH, D_MIX, D_MODEL), f32) * D_MIX ** -0.5,
    }


def reference(x_prompt, x_sample, state_rglru_conv, state_rglru_h, pre_norm_g, post_norm_g,
              w_in, conv_w, conv_b, w_rgate, b_rgate, w_igate, b_igate, lru_lambda,
              sgu_norm_g, w_spatial, b_spatial, w_out):
    yp = x_prompt
    ys = x_sample
    conv_p, h_p, conv_s, h_s, v_s = [], [], [], [], []
    for l in range(DEPTH):
        params = (pre_norm_g[l], post_norm_g[l], w_in[l], conv_w[l], conv_b[l],
                  w_rgate[l], b_rgate[l], w_igate[l], b_igate[l], lru_lambda[l],
                  sgu_norm_g[l], w_spatial[l], b_spatial[l], w_out[l])
        buf0 = jnp.zeros((yp.shape[0], CONV_WIDTH - 1, LRU_WIDTH), yp.dtype)
        hz = jnp.zeros((yp.shape[0], LRU_WIDTH), state_rglru_h.dtype)
        yp, cb_p, hl_p, _ = mixer_layer(yp, 0, buf0, hz, *params)
        ys, cb_s, hl_s, vs = mixer_layer(ys, PAST_LEN, state_rglru_conv[l], state_rglru_h[l], *params)
        conv_p.append(cb_p)
        h_p.append(hl_p)
        conv_s.append(cb_s)
        h_s.append(hl_s)
        v_s.append(vs)
    return (yp, ys, jnp.stack(conv_p), jnp.stack(h_p), jnp.stack(conv_s), jnp.stack(h_s), jnp.stack(v_s))
```

```python
from contextlib import ExitStack
import numpy as np
import concourse.bass as bass
import concourse.mybir as mybir
from concourse.bass_utils import run_bass_kernel_spmd

F32 = mybir.dt.float32
BF16 = mybir.dt.bfloat16
AF = mybir.ActivationFunctionType
ALU = mybir.AluOpType

ENGS = ("pe", "act", "dve", "pool", "sp")

D = 2048
NMAIN = 1024
NSMP = 128
NM = NMAIN + NSMP
NZ = NM + 4
NXR = 1280
EPS = 1e-6
GK = 0.7978845608028654
GC = 0.044715


class Sched:
    def __init__(self):
        self.ops = []
        self.last_w = {}
        self.readers = {}
        self.dma_count = {}
        self.reg_cur = {}
        self.reg_prev = {}

    def new_epoch(self, region):
        self.reg_prev[region] = self.reg_prev.get(region, set()) | self.reg_cur.get(region, set())
        self.reg_cur[region] = set()

    def add(self, eng, fn, reads=(), writes=(), dma=None, regions=(), inc=16):
        idx = len(self.ops)
        deps = set()
        for r in reads:
            if r in self.last_w:
                deps.add(self.last_w[r])
        for w in writes:
            if w in self.last_w:
                deps.add(self.last_w[w])
            for rd in self.readers.get(w, ()):
                deps.add(rd)
        for rg in regions:
            deps |= self.reg_prev.get(rg, set())
            self.reg_cur.setdefault(rg, set()).add(idx)
        deps.discard(idx)
        op = dict(eng=eng, fn=fn, deps=deps, dma=dma, idx=idx, ms=None)
        if dma is not None:
            self.dma_count[dma] = self.dma_count.get(dma, 0) + inc
            op["dma_ord"] = self.dma_count[dma]
            op["inc"] = inc
        self.ops.append(op)
        for r in reads:
            self.readers.setdefault(r, []).append(idx)
        for w in writes:
            self.last_w[w] = idx
            self.readers[w] = []
        return idx

    def emit(self, nc, stack):
        ops = self.ops
        need = set()
        for op in ops:
            for d in op["deps"]:
                a = ops[d]
                if a["dma"] is None:
                    if a["eng"] == "pe" and op["eng"] == "pe" and op["dma"] is None:
                        continue
                    need.add(d)
        cnt = {e: 0 for e in ENGS}
        for op in ops:
            if op["dma"] is None and op["idx"] in need:
                cnt[op["eng"]] += 1
                op["ms"] = cnt[op["eng"]]
        sem_eng = {e: stack.enter_context(nc.semaphore("s_" + e)) for e in ENGS if cnt[e] > 0}
        sem_dma = {k: stack.enter_context(nc.semaphore("d_" + str(k))) for k in self.dma_count}
        block = stack.enter_context(nc.Block())

        def run(eng_name, eng):
            waited = {}
            for op in ops:
                if op["eng"] != eng_name:
                    continue
                wl = {}
                for d in op["deps"]:
                    a = ops[d]
                    if a["dma"] is not None:
                        key = ("d", a["dma"])
                        val = a["dma_ord"]
                    else:
                        if a["eng"] == "pe" and eng_name == "pe" and op["dma"] is None:
                            continue
                        key = ("e", a["eng"])
                        val = a["ms"]
                    if wl.get(key, 0) < val:
                        wl[key] = val
                for key, val in wl.items():
                    if waited.get(key, 0) >= val:
                        continue
                    waited[key] = val
                    sem = sem_dma[key[1]] if key[0] == "d" else sem_eng[key[1]]
                    eng.wait_ge(sem, val)
                if op["fn"] is None:
                    continue
                ins = op["fn"](eng)
                if op["dma"] is not None:
                    ins.then_inc(sem_dma[op["dma"]], op["inc"])
                elif op["ms"] is not None:
                    ins.then_inc(sem_eng[eng_name], 1)

        @block.tensor
        def _(e):
            run("pe", e)

        @block.scalar
        def _(e):
            run("act", e)

        @block.vector
        def _(e):
            run("dve", e)

        @block.gpsimd
        def _(e):
            run("pool", e)

        @block.sync
        def _(e):
            run("sp", e)


def build_nc():
    nc = bass.Bass("TRN2", target_bir_lowering=False)

    def din(name, shape):
        return nc.dram_tensor(name, list(shape), F32, kind="ExternalInput").ap()

    def dout(name, shape):
        return nc.dram_tensor(name, list(shape), F32, kind="ExternalOutput").ap()

    x_d = din("x", [NXR, D])
    w40_d = din("w40", [40, 128, 16, 128])
    wout_d = din("wout", [128, 16, D])
    wv_d = din("wv", [128, 16, 1024])
    cpk_d = din("cpk", [128, 64])
    cw_d = din("cw", [128, 8, 4])
    wr_d = din("wr", [128, 8, 128])
    wi_d = din("wi", [128, 8, 128])
    wsT_d = din("wsT", [128, 8, 128])
    wsS_d = din("wsS", [128, 8, 128])
    mask_d = din("mask", [128, 128])
    maskS_d = din("maskS", [128, 128])
    idn_d = din("idn", [128, 128])
    bsp_d = din("bsp", [1, 2048])
    sgug_d = din("sgug", [128, 1024])
    postg_d = din("postg", [128, D])
    pregt_d = din("pregt", [128, D])
    cst_d = din("cst", [128, 8, 16, 3])
    hst_d = din("hst", [128, 8, 16])
    y_d = dout("y", [NM, D])
    vs_d = dout("vs", [128, 1024])
    st_d = dout("st", [128, 8, 68])
    hx_in = nc.dram_tensor("hx_in", [128, 8], F32)
    hx_out = nc.dram_tensor("hx_out", [256, 8], F32)

    S = Sched()
    A = S.add
    with ExitStack() as st:
        def sb(name, shape, dt):
            return st.enter_context(nc.sbuf_tensor("sb_" + name, list(shape), dt))

        zTp = sb("zTp", [128, 16, 1024], BF16)
        zTm = sb("zTm", [128, 16, NZ], BF16)
        cat = sb("cat", [128, 16, NM], BF16)
        R3 = sb("R3", [128, 16384], F32)
        vn = sb("vn", [128, 9, 1024], BF16)
        cpk = sb("cpk", [128, 64], F32)
        cw = sb("cw", [128, 8, 4], F32)
        der = sb("der", [128, 64], F32)
        wr_b = sb("wr_b", [128, 8, 128], BF16)
        wi_b = sb("wi_b", [128, 8, 128], BF16)
        wsT_b = sb("wsT_b", [128, 8, 128], BF16)
        wsS_b = sb("wsS_b", [128, 8, 128], BF16)
        idb = sb("idb", [128, 128], BF16)
        biasK = sb("biasK", [128, 2048], BF16)
        onesK = sb("onesK", [128, 128], BF16)
        stt = sb("stt", [128, 8, 68], F32)
        cst = sb("cst", [128, 8, 16, 3], F32)
        hst = sb("hst", [128, 8, 16], F32)
        stat = sb("stat", [128, 64], F32)
        mhalf = sb("mhalf", [128, 1], F32)
        st1 = sb("st1", [128, 3 * 10], F32)
        st3 = sb("st3", [128, 4 * 9], F32)
        st5 = sb("st5", [128, 3 * 9], F32)
        hx = sb("hx", [128, 8], F32)
        pend = sb("pend", [128, 8], F32)
        hin = sb("hin", [128, 8], F32)
        ps = st.enter_context(nc.psum_tensor("ps", [128, 8, 512], F32))

        def psflat(b0, n):
            return ps[:, b0:b0 + (n + 511) // 512, :].rearrange("p a b -> p (a b)")[:, 0:n]

        zpf = zTp[:, :, :].rearrange("p a b -> p (a b)")
        zpf32 = zpf.bitcast(F32)
        zmf32 = zTm[:, :, :].rearrange("p a b -> p (a b)").bitcast(F32)
        vnf = vn[:, :, :].rearrange("p a b -> p (a b)")
        catf = cat[:, 8:16, :].rearrange("p a b -> p (a b)").bitcast(F32)

        C_CB, C_BR, C_BI, C_LAM, C_FLAG = 0, 8, 16, 24, 32
        D_HBR, D_HBI, D_HS8, D_QS8, D_FLA, D_FLB, D_T0, D_T1, D_T2 = 0, 8, 16, 24, 32, 33, 34, 42, 50

        RG = ["R3"]
        CS = ["CS"]
        ZM = ["ZM"]
        CL = ["CL"]
        NXS = 4
        xt = [R3[:, i * 2048:(i + 1) * 2048] for i in range(NXS)]
        ztb = [R3[:, 8192 + i * 1024:8192 + (i + 1) * 1024].bitcast(BF16) for i in range(3)]
        pregt = R3[:, 11264:13312]
        s_idn = catf[:, 0:128]
        vb = [catf[:, 0:1024], catf[:, 1024:2048]]
        sgug = catf[:, 2048:3072]
        vjunk = catf[:, 3072:3584].bitcast(BF16)
        sqj = catf[:, 3584:4608].bitcast(BF16)
        Wv = zTp
        WVALL = [("Wv", j) for j in range(4)]
        A("sp", lambda e: e.dma_start(out=s_idn, in_=idn_d), writes=["s_idn"], dma="c9", regions=CS)
        A("sp", lambda e: e.dma_start(out=xt[0], in_=x_d[0:128, :]), writes=["xt0"], dma="xt0", regions=RG)
        A("sp", lambda e: e.dma_start(out=pregt, in_=pregt_d), writes=["pregt"], dma="c11", regions=RG)
        for t in range(1, NXS):
            A("sp", lambda e, t=t: e.dma_start(out=xt[t], in_=x_d[t * 128:(t + 1) * 128, :]), writes=[f"xt{t}"], dma=f"xt{t}", regions=RG)

        def load_wv():
            for j in range(4):
                A("pool", lambda e, j=j: e.dma_start(out=Wv[:, 4 * j:4 * j + 4, :], in_=wv_d[:, 4 * j:4 * j + 4, :]), writes=[("Wv", j)], dma="wv", regions=["ZP"])
        A("dve", lambda e: e.memset(mhalf[:], -0.5), writes=["mhalf"])
        A("dve", lambda e: e.tensor_copy(out=idb[:], in_=s_idn), reads=["s_idn"], writes=["idb"], regions=CS)

        def zT_tokens(lo, hi):
            toks = []
            for t in range(lo // 128, (hi + 127) // 128):
                toks += [("zT", t, 0), ("zT", t, 1)]
            return toks

        def p1_a(t):
            sl = t % NXS
            z2 = t % 3
            c0 = 3 * t
            if t >= NXS:
                nrow = 128 if t < 9 else 3
                A("sp", lambda e, t=t, sl=sl, nrow=nrow: e.dma_start(out=xt[sl][0:nrow, :], in_=x_d[t * 128:t * 128 + nrow, :]), writes=[f"xt{sl}"], dma=f"xt{sl}", regions=RG)
            A("act", lambda e, sl=sl, c0=c0: e.activation(out=sqj, in_=xt[sl], func=AF.Square, accum_out=st1[:, c0:c0 + 1]),
              reads=[f"xt{sl}"], writes=["sqj", ("s1", t, 0)], regions=RG + CS)
            A("dve", lambda e, c0=c0: e.tensor_scalar(out=st1[:, c0 + 1:c0 + 2], in0=st1[:, c0:c0 + 1], scalar1=1.0 / D, scalar2=EPS, op0=ALU.mult, op1=ALU.add),
              reads=[("s1", t, 0)], writes=[("s1", t, 1)])
            A("pool", lambda e, c0=c0: e.tensor_tensor(out=st1[:, c0 + 2:c0 + 3], in0=st1[:, c0 + 1:c0 + 2], in1=mhalf[:], op=ALU.pow),
              reads=[("s1", t, 1), "mhalf"], writes=[("s1", t, 2)])

        def p1_a2(t):
            sl = t % NXS
            z2 = t % 3
            c0 = 3 * t
            A("dve", lambda e, sl=sl, z2=z2, c0=c0: e.scalar_tensor_tensor(out=ztb[z2], in0=xt[sl], scalar=st1[:, c0 + 2:c0 + 3], in1=pregt, op0=ALU.mult, op1=ALU.mult),
              reads=[f"xt{sl}", ("s1", t, 2), "pregt"], writes=[f"zt{z2}"], regions=RG)

        def p1_b(t):
            z2 = t % 3
            pb = 2 * (t % 2)
            psb = ps[:, pb:pb + 2, :].rearrange("p a b -> p (a b)").bitcast(BF16)

            def trf(e, z2=z2, psb=psb):
                ins = None
                for kc in range(16):
                    ins = e.transpose(out=psb[:, kc * 128:(kc + 1) * 128], in_=ztb[z2][:, kc * 128:(kc + 1) * 128], identity=idb[:])
                return ins
            A("pe", trf, reads=[f"zt{z2}", "idb"], writes=[("ps", pb), ("ps", pb + 1)], regions=RG)
            zo = t * 128
            nn = 128 if t < 9 else 3
            A("act", lambda e, zo=zo, nn=nn, psb=psb: e.activation(out=zTm[:, 0:8, zo:zo + nn], in_=psb[:, 0:1024].rearrange("p (a b) -> p a b", a=8)[:, :, 0:nn], func=AF.Copy),
              reads=[("ps", pb)], writes=[("zT", t, 0)], regions=ZM)
            A("dve", lambda e, zo=zo, nn=nn, psb=psb: e.tensor_copy(out=zTm[:, 8:16, zo:zo + nn], in_=psb[:, 1024:2048].rearrange("p (a b) -> p a b", a=8)[:, :, 0:nn]),
              reads=[("ps", pb + 1)], writes=[("zT", t, 1)], regions=ZM)

        def v_tile(ti):
            tok = ti * 128
            s = ti % 2
            bk = [2 * ti, 2 * ti + 1] if ti < 2 else [4 + 2 * s, 5 + 2 * s]

            def vf(e, tok=tok, bk=bk):
                ins = None
                for kc in range(16):
                    for half in range(2):
                        ins = e.matmul(ps[:, bk[half], :], lhsT=zTm[:, kc, tok:tok + 128], rhs=Wv[:, kc, half * 512:(half + 1) * 512],
                                       start=(kc == 0), stop=(kc == 15))
                return ins
            A("pe", vf, reads=WVALL + zT_tokens(tok, tok + 128), writes=[("ps", b) for b in bk], regions=["ZP", "ZM"])
            PSV = [("ps", b) for b in bk]
            vps = psflat(bk[0], 1024)
            A("act", lambda e, s=s, vps=vps: e.activation(out=vb[s], in_=vps, func=AF.Square), reads=PSV, writes=[("vb", s)], regions=CS)
            A("dve", lambda e, s=s, vps=vps: e.scalar_tensor_tensor(out=vb[s], in0=vb[s], scalar=1.0 / GC, in1=vps, op0=ALU.add, op1=ALU.mult),
              reads=[("vb", s)] + PSV, writes=[("vb", s)], regions=CS)
            A("act", lambda e, s=s: e.activation(out=vb[s], in_=vb[s], func=AF.Tanh, scale=GK * GC), reads=[("vb", s)], writes=[("vb", s)], regions=CS)
            A("dve", lambda e, s=s, vps=vps: e.scalar_tensor_tensor(out=vb[s], in0=vb[s], scalar=1.0, in1=vps, op0=ALU.add, op1=ALU.mult),
              reads=[("vb", s)] + PSV, writes=[("vb", s)] + PSV, regions=CS)
            c0 = 4 * ti
            A("act", lambda e, s=s, c0=c0: e.activation(out=vjunk, in_=vb[s], func=AF.Square, accum_out=st3[:, c0:c0 + 1]), reads=[("vb", s)], writes=["vjunk", ("s3", ti, 0)], regions=CS)
            A("dve", lambda e, c0=c0: e.tensor_scalar(out=st3[:, c0 + 1:c0 + 2], in0=st3[:, c0:c0 + 1], scalar1=0.25 / 1024, scalar2=EPS, op0=ALU.mult, op1=ALU.add),
              reads=[("s3", ti, 0)], writes=[("s3", ti, 1)])
            A("pool", lambda e, c0=c0: e.tensor_tensor(out=st3[:, c0 + 2:c0 + 3], in0=st3[:, c0 + 1:c0 + 2], in1=mhalf[:], op=ALU.pow), reads=[("s3", ti, 1), "mhalf"], writes=[("s3", ti, 2)])
            A("dve", lambda e, c0=c0: e.tensor_scalar(out=st3[:, c0 + 3:c0 + 4], in0=st3[:, c0 + 2:c0 + 3], scalar1=0.5, scalar2=None, op0=ALU.mult), reads=[("s3", ti, 2)], writes=[("s3", ti, 3)])
            rs = st3[:, c0 + 3:c0 + 4]
            if ti < 8:
                A("dve", lambda e, s=s, ti=ti, rs=rs: e.scalar_tensor_tensor(out=vn[:, ti, :], in0=vb[s], scalar=rs, in1=sgug, op0=ALU.mult, op1=ALU.mult),
                  reads=[("vb", s), ("s3", ti, 3), "sgug"], writes=[("vn", ti)], regions=CS + ["VN"])
            else:
                A("dve", lambda e, s=s, rs=rs: e.scalar_tensor_tensor(out=vb[s], in0=vb[s], scalar=rs, in1=sgug, op0=ALU.mult, op1=ALU.mult),
                  reads=[("vb", s), ("s3", ti, 3), "sgug"], writes=[("vb", s)], regions=CS)
                A("sp", lambda e, s=s: e.dma_start(out=vs_d, in_=vb[s]), reads=[("vb", s)], dma="o_vs", regions=CS)
                A("act", lambda e, s=s, ti=ti: e.activation(out=vn[:, ti, :], in_=vb[s], func=AF.Copy), reads=[("vb", s)], writes=[("vn", ti)], regions=CS + ["VN"])

        def const_a():
            A("act", lambda e: e.dma_start(out=cpk[:], in_=cpk_d), writes=["cpk"], dma="c0")
            A("act", lambda e: e.dma_start(out=cw[:], in_=cw_d), writes=["cw"], dma="c1")
            A("act", lambda e: e.dma_start(out=cst[:], in_=cst_d), writes=["cst"], dma="c2")
            A("act", lambda e: e.dma_start(out=hst[:], in_=hst_d), writes=["hst"], dma="c3")
            A("dve", lambda e: e.memset(stt[:], 0.0), writes=["stt"])
            T0 = der[:, D_T0:D_T0 + 8]
            T1 = der[:, D_T1:D_T1 + 8]
            T2 = der[:, D_T2:D_T2 + 8]
            lam = cpk[:, C_LAM:C_LAM + 8]
            HS8 = der[:, D_HS8:D_HS8 + 8]
            QS8 = der[:, D_QS8:D_QS8 + 8]
            A("dve", lambda e: e.tensor_scalar(out=T0, in0=lam, scalar1=-1.0, scalar2=None, op0=ALU.mult), reads=["cpk"], writes=["T0"])
            A("dve", lambda e: e.tensor_tensor(out=T1, in0=T0, in1=lam, op=ALU.min), reads=["T0", "cpk"], writes=["T1"])
            A("act", lambda e: e.activation(out=T1, in_=T1, func=AF.Exp), reads=["T1"], writes=["T1"])
            A("dve", lambda e: e.tensor_scalar(out=T2, in0=T1, scalar1=1.0, scalar2=None, op0=ALU.add), reads=["T1"], writes=["T2"])
            A("act", lambda e: e.activation(out=HS8, in_=T2, func=AF.Ln), reads=["T2"], writes=["HS8"])
            A("dve", lambda e: e.tensor_scalar(out=T2, in0=T2, scalar1=-1.0, scalar2=1e-30, op0=ALU.add, op1=ALU.max), reads=["T2", "HS8"], writes=["T2"])
            A("dve", lambda e: e.reciprocal(out=T2, in_=T2), reads=["T2"], writes=["T2"])
            A("dve", lambda e: e.tensor_tensor(out=T1, in0=T1, in1=T2, op=ALU.mult), reads=["T1", "T2"], writes=["T1"])
            A("dve", lambda e: e.tensor_tensor(out=T1, in0=T1, in1=HS8, op=ALU.mult), reads=["T1", "HS8"], writes=["T1"])
            A("dve", lambda e: e.tensor_scalar(out=T0, in0=T0, scalar1=0.0, scalar2=None, op0=ALU.max), reads=["T0"], writes=["T0"])
            A("dve", lambda e: e.tensor_tensor(out=T1, in0=T1, in1=T0, op=ALU.add), reads=["T1", "T0"], writes=["T1"])
            A("dve", lambda e: e.tensor_scalar(out=HS8, in0=T1, scalar1=-4.0, scalar2=None, op0=ALU.mult), reads=["T1"], writes=["HS8"])
            A("dve", lambda e: e.tensor_scalar(out=QS8, in0=T1, scalar1=2.0, scalar2=None, op0=ALU.mult), reads=["T1"], writes=["QS8"])
            A("dve", lambda e: e.tensor_scalar(out=der[:, D_HBR:D_HBR + 16], in0=cpk[:, C_BR:C_BR + 16], scalar1=0.5, scalar2=None, op0=ALU.mult),
              reads=["cpk"], writes=["HB"])
            A("dve", lambda e: e.tensor_scalar(out=der[:, D_FLA:D_FLA + 2], in0=cpk[:, C_FLAG:C_FLAG + 2], scalar1=0.5, scalar2=None, op0=ALU.mult),
              reads=["cpk"], writes=["FL"])
            return ["HS8", "QS8", "HB", "FL", "cpk", "cw"]

        def const_b():
            catl = cat[:, 0:8, :].rearrange("p a b -> p (a b)").bitcast(F32)
            ZP_ = ["ZP"]
            s_wr = zpf32[:, 0:1024].rearrange("p (a b) -> p a b", a=8)
            s_wi = zpf32[:, 1024:2048].rearrange("p (a b) -> p a b", a=8)
            s_wsT = zpf32[:, 2048:3072].rearrange("p (a b) -> p a b", a=8)
            s_wsS = zpf32[:, 3072:4096].rearrange("p (a b) -> p a b", a=8)
            s_mask = zpf32[:, 4096:4224]
            s_maskS = zpf32[:, 7296:7424]
            s_b = zpf32[0:1, 4224:6272]
            s_h = zpf32[0:1, 6272:7296].bitcast(BF16)
            s_l = zpf32[0:1, 0:1024].bitcast(BF16)
            A("sp", lambda e: e.dma_start(out=s_wr, in_=wr_d), writes=["s_wr"], dma="c4", regions=ZP_)
            A("sp", lambda e: e.dma_start(out=s_wi, in_=wi_d), writes=["s_wi"], dma="c5", regions=ZP_)
            A("sp", lambda e: e.dma_start(out=s_wsT, in_=wsT_d), writes=["s_wsT"], dma="c6", regions=ZP_)
            A("sp", lambda e: e.dma_start(out=s_wsS, in_=wsS_d), writes=["s_wsS"], dma="c7", regions=ZP_)
            A("sp", lambda e: e.dma_start(out=s_mask, in_=mask_d), writes=["s_mask"], dma="c8", regions=ZP_)
            A("sp", lambda e: e.dma_start(out=s_maskS, in_=maskS_d), writes=["s_maskS"], dma="c8s", regions=ZP_)
            A("sp", lambda e: e.dma_start(out=s_b, in_=bsp_d), writes=["s_b"], dma="c10", regions=ZP_)

            def p0():
                A("act", lambda e: e.activation(out=wr_b[:], in_=s_wr, func=AF.Copy), reads=["s_wr"], writes=["wr_b"], regions=ZP_)
                A("act", lambda e: e.activation(out=wi_b[:], in_=s_wi, func=AF.Copy), reads=["s_wi"], writes=["wi_b"], regions=ZP_)

            def p1():
                A("dve", lambda e: e.memset(biasK[:], 0.0), writes=["biasK"])
                A("dve", lambda e: e.memset(onesK[:], 0.0), writes=["onesK"])
                A("dve", lambda e: e.memset(onesK[0:2, :], 1.0), writes=["onesK"])

            def p1b():
                A("act", lambda e: e.activation(out=s_h, in_=s_b, func=AF.Copy), reads=["s_b"], writes=["s_h"], regions=ZP_)

            def p2():
                A("dve", lambda e: e.tensor_tensor(out=s_b, in0=s_b, in1=s_h, op=ALU.subtract), reads=["s_b", "s_h"], writes=["s_b"], regions=ZP_)

            def p2b():
                A("act", lambda e: e.activation(out=s_l, in_=s_b, func=AF.Copy), reads=["s_b"], writes=["s_l", "s_wr"], regions=ZP_)
                A("sp", lambda e: e.dma_start(out=biasK[0:1, :], in_=s_h), reads=["s_h", "biasK"], writes=["biasK", "constCL"], dma="c14", regions=ZP_)
                A("sp", lambda e: e.dma_start(out=biasK[1:2, :], in_=s_l), reads=["s_l", "biasK"], writes=["biasK", "constCL"], dma="c15", regions=ZP_)

            def p3():
                for hh in range(4):
                    A("dve", lambda e, hh=hh: e.tensor_tensor(out=wsT_b[:, hh, :], in0=s_wsT[:, hh, :], in1=s_mask, op=ALU.mult),
                      reads=["s_wsT", "s_mask"], writes=[("wsT_b", hh)], regions=ZP_)

            def p3b():
                for hh in range(4, 8):
                    A("dve", lambda e, hh=hh: e.tensor_tensor(out=wsT_b[:, hh, :], in0=s_wsT[:, hh, :], in1=s_mask, op=ALU.mult),
                      reads=["s_wsT", "s_mask"], writes=[("wsT_b", hh)], regions=ZP_)
                for hh in range(8):
                    A("dve", lambda e, hh=hh: e.tensor_tensor(out=wsS_b[:, hh, :], in0=s_wsS[:, hh, :], in1=s_maskS, op=ALU.mult),
                      reads=["s_wsS", "s_maskS"], writes=[("wsS_b", hh)] + (["constB"] if hh == 7 else []), regions=ZP_)
            return [p0, p3, p3b, p1, p1b, p2, p2b]


        CONST = const_a()
        for t in range(12):
            if t < 10:
                p1_a(t)
            if 0 <= t - 2 < 10:
                p1_b(t - 2)
            if t < 10:
                p1_a2(t)
            if t in (6, 9):
                for k, chunk in enumerate((0, 1, 8)):
                    if (k == 0) != (t == 6):
                        continue
                    dst = R3[:, 13312 + 1024 * k:13312 + 1024 * (k + 1)].bitcast(BF16).rearrange("p (a b) -> p a b", a=16)
                    A("pool", lambda e, chunk=chunk, dst=dst: e.dma_start(out=dst, in_=w40_d[chunk]), writes=[("eslab", k)], dma="sl_eslab_%d" % k)
        RG = ["R3"]

        slab_ctr = [0]

        SET2 = [("a", 2), ("tq", 2), ("m2", 2), ("xc", 2)]

        def slab_view(k):
            if k < 3:
                return R3[:, 13312 + 1024 * k:13312 + 1024 * (k + 1)].bitcast(BF16), [], ("eslab", k), []
            if k < 16:
                sl = k % 3
                return vnf[:, sl * 2048:(sl + 1) * 2048], ["VN"], ("slab", sl), []
            if k < 19:
                j = k - 16
                return R3[:, 6912 + 1024 * j:6912 + 1024 * (j + 1)].bitcast(BF16), ["R3"], ("pslab", j), SET2
            sl = k % 3
            return slab4[:, sl * 2048:(sl + 1) * 2048], ["ZP"], ("slab4", sl), []

        def load_slab_k(k, chunk):
            view, reg, tok, extra = slab_view(k)
            dst = view.rearrange("p (a b) -> p a b", a=16)
            A("pool", lambda e, chunk=chunk, dst=dst: e.dma_start(out=dst, in_=w40_d[chunk]), writes=[tok] + extra, dma="sl_%s_%s" % tok, regions=reg)

        def fjob(k, blocks, banks, split=False):
            view, reg, tok, _ = slab_view(k)
            groups = [[(bl, b)] for bl, b in zip(blocks, banks)] if split else [list(zip(blocks, banks))]
            for grp in groups:
                def f(e, grp=grp):
                    ins = None
                    for kc in range(16):
                        for (lo, n), b in grp:
                            ins = e.matmul(ps[:, b, 0:n], lhsT=view[:, kc * 128:(kc + 1) * 128], rhs=zTm[:, kc, lo:lo + n],
                                           start=(kc == 0), stop=(kc == 15))
                    return ins
                rd = [tok]
                for (lo, n), b in grp:
                    rd += zT_tokens(lo, lo + n)
                A("pe", f, reads=rd, writes=[("ps", b) for _, b in grp], regions=list(reg) + ZM)

        slab_seq = [0]
        for h in range(1, 8):
            slab_seq += [h, 8 + h - 1]
        slab_seq += [15]
        for h in range(8):
            slab_seq += [16 + h, 32 + h]
        sq_loaded = [3]
        sq_taken = [0]
        after_load = [None]

        def take_slab(chunk):
            k = sq_taken[0]
            assert slab_seq[k] == chunk, (k, chunk, slab_seq[k])
            while sq_loaded[0] < len(slab_seq) and sq_loaded[0] <= k + 2:
                load_slab_k(sq_loaded[0], slab_seq[sq_loaded[0]])
                sq_loaded[0] += 1
                if after_load[0] is not None:
                    after_load[0]()
            sq_taken[0] += 1
            return k

        BLK_M = [(0, 512), (512, 512), (1024, 128)]
        BLK_X = [(0, 512), (512, 512), (1024, 131)]

        S.new_epoch("R3")
        S.new_epoch("CL")
        S.new_epoch("CS")
        ZP = ["ZP"]
        slab4 = zpf[:, 9216:9216 + 6144]
        BIASC = ["biasK", "onesK"]
        constb_pieces = const_b()
        NS3 = 3
        abuf = [R3[:, s * 3456:s * 3456 + 1152] for s in range(NS3)]
        tqb = [R3[:, s * 3456 + 1152:s * 3456 + 2304] for s in range(NS3)]
        m2b = [R3[:, s * 3456 + 2304:s * 3456 + 3456] for s in range(NS3)]
        xcbb = [R3[:, 10368 + s * 576:10368 + (s + 1) * 576].bitcast(BF16) for s in range(NS3)]
        Qb = R3[:, 12096:12096 + 4096].bitcast(BF16).rearrange("p (a b) -> p a b", a=8)
        thbb = [catf[:, 0:1152], catf[:, 1152:2304]]
        xr = catf[:, 2304:2304 + 1203]
        xr_s = xr[:, 1027:1203].rearrange("p (s r) -> p s r", r=11)

        def x_pe(h):
            fjob(take_slab(h), BLK_X, [0, 1, 2], split=(h == 0))

        def x_evac(h):
            s = h % NS3
            ab = abuf[s]
            w3 = cw[:, h, 3:4]
            cb = cpk[:, C_CB + h:C_CB + h + 1]
            XC = [("xc", s)]
            PX = [("ps", 0), ("ps", 1), ("ps", 2)]
            A("act", lambda e, h=h: e.activation(out=xr_s[:, :, 0:3], in_=cst[:, h, :, :], func=AF.Copy), reads=["cst"], writes=["xr"], regions=CS)
            A("act", lambda e: e.activation(out=xr[:, 3:1027], in_=psflat(0, 1024), func=AF.Copy), reads=PX, writes=["xr"], regions=CS)
            A("act", lambda e: e.activation(out=xr_s[:, :, 3:11], in_=ps[:, 2, 0:128].rearrange("p (s t) -> p s t", t=8), func=AF.Copy),
              reads=PX, writes=["xr"], regions=CS)
            A("act", lambda e: e.activation(out=xr[:, 0:3], in_=ps[:, 2, 128:131], func=AF.Copy), reads=PX, writes=["xr"], regions=CS)
            A("act", lambda e, ab=ab, w3=w3, cb=cb: e.activation(out=ab[:, 0:NM], in_=psflat(0, NM), func=AF.Identity, scale=w3, bias=cb),
              reads=PX + CONST, writes=XC + [("a", s)], regions=RG)

        def x_taps(h):
            s = h % NS3
            ab = abuf[s]
            XC = [("xc", s)]
            xc_s = ab[:, 1024:NM].rearrange("p (s t) -> p s t", t=8)
            for k in range(3):
                A("dve", lambda e, k=k, h=h, ab=ab: e.scalar_tensor_tensor(out=ab[:, 0:1024], in0=xr[:, k:k + 1024], scalar=cw[:, h, k:k + 1], in1=ab[:, 0:1024],
                                                                        op0=ALU.mult, op1=ALU.add), reads=["xr"] + XC + CONST, writes=XC, regions=RG + CS)
                A("dve", lambda e, k=k, h=h, xc_s=xc_s: e.scalar_tensor_tensor(out=xc_s, in0=xr_s[:, :, k:k + 8], scalar=cw[:, h, k:k + 1], in1=xc_s,
                                                                            op0=ALU.mult, op1=ALU.add), reads=["xr"] + XC + CONST, writes=XC, regions=RG + CS)

        def x_cast(h):
            s = h % NS3
            ab, xcb = abuf[s], xcbb[s]
            XC = [("xc", s)]
            A("act", lambda e, ab=ab, xcb=xcb: e.activation(out=xcb[:, 0:NM], in_=ab[:, 0:NM], func=AF.Copy), reads=XC, writes=[("xcb", s)], regions=RG)
            A("act", lambda e, h=h: e.activation(out=stt[:, h, 0:3], in_=xr[:, 1024:1027], func=AF.Copy), reads=["xr", "stt"], writes=[("stt", h)], regions=CS)
            A("act", lambda e, h=h: e.activation(out=stt[:, h, 4:52].rearrange("p (s r) -> p s r", r=3), in_=xr_s[:, :, 8:11], func=AF.Copy),
              reads=["xr", "stt"], writes=[("stt", h)], regions=CS)

        GBLK = [(0, 512), (512, 512), (1024, 128)]

        def g_gates(h):
            s = h % NS3
            tq, m2, xcb = tqb[s], m2b[s], xcbb[s]
            hbr = der[:, D_HBR + h:D_HBR + h + 1]
            hbi = der[:, D_HBI + h:D_HBI + h + 1]
            j = 0
            for (lo, n) in GBLK:
                for (wmat, wname, dst, dname, bias) in ((wr_b, "wr_b", tq, "tqp", hbr), (wi_b, "wi_b", m2, "m2p", hbi)):
                    bank = 3 + (j % 2)
                    j += 1
                    A("pe", lambda e, lo=lo, n=n, bank=bank, wmat=wmat, xcb=xcb, h=h: e.matmul(ps[:, bank, 0:n], lhsT=wmat[:, h, :], rhs=xcb[:, lo:lo + n], start=True, stop=True),
                      reads=[("xcb", s), wname], writes=[("ps", bank)], regions=RG)
                    A("act", lambda e, lo=lo, n=n, bank=bank, dst=dst, bias=bias: e.activation(out=dst[:, lo:lo + n], in_=ps[:, bank, 0:n], func=AF.Tanh, scale=0.5, bias=bias),
                      reads=[("ps", bank)] + CONST, writes=[(dname, s, lo)] + ([("tq", s) if dname == "tqp" else ("m2", s)] if lo == 0 else []), regions=RG)

        def g_m2(h):
            s = h % NS3
            ab, m2 = abuf[s], m2b[s]
            XC = [("xc", s)]
            M2P = [("m2p", s, lo) for lo, _ in GBLK]
            A("dve", lambda e, m2=m2, ab=ab: e.scalar_tensor_tensor(out=m2[:, 0:NM], in0=m2[:, 0:NM], scalar=1.0, in1=ab[:, 0:NM], op0=ALU.add, op1=ALU.mult),
              reads=M2P + XC, writes=[("m2", s)], regions=RG)

        def g_exp(h):
            s = h % NS3
            ab, tq = abuf[s], tqb[s]
            hs8 = der[:, D_HS8 + h:D_HS8 + h + 1]
            qs8 = der[:, D_QS8 + h:D_QS8 + h + 1]
            XC = [("xc", s)]
            TQP = [("tqp", s, lo) for lo, _ in GBLK]
            A("act", lambda e, ab=ab, tq=tq, hs8=hs8: e.activation(out=ab[:, 0:NM], in_=tq[:, 0:NM], func=AF.Exp, scale=hs8, bias=hs8),
              reads=TQP + [("m2", s)] + CONST, writes=XC + [("a", s)], regions=RG)
            A("act", lambda e, tq=tq, qs8=qs8: e.activation(out=tq[:, 0:NM], in_=tq[:, 0:NM], func=AF.Tanh, scale=qs8, bias=qs8),
              reads=TQP + CONST, writes=[("tq", s)] + TQP, regions=RG)

        def g_gr(h):
            t2 = h % 2
            thb = thbb[t2]
            bkg = [5, 6, 7]
            fjob(take_slab(8 + h), BLK_M, bkg)
            A("act", lambda e, thb=thb: e.activation(out=thb, in_=psflat(5, NM), func=AF.Silu),
              reads=[("ps", b) for b in bkg], writes=[("thb", t2)], regions=CS)

        def c_sqrt(h):
            s = h % NS3
            tq = tqb[s]
            A("act", lambda e, tq=tq: e.activation(out=tq[:, 0:NM], in_=tq[:, 0:NM], func=AF.Sqrt), reads=[("tq", s)], writes=[("tq", s)], regions=RG)

        def c_main(h):
            s = h % NS3
            ab, tq, m2 = abuf[s], tqb[s], m2b[s]
            TQ = [("tq", s)]
            M2 = [("m2", s)]
            AA = [("a", s)]
            fla = der[:, D_FLA:D_FLA + 1]
            flb = der[:, D_FLB:D_FLB + 1]
            A("dve", lambda e, ab=ab, tq=tq: e.scalar_tensor_tensor(out=tq[:, 0:NM], in0=ab[:, 0:NM], scalar=1.0, in1=tq[:, 0:NM], op0=ALU.add, op1=ALU.mult),
              reads=TQ + AA, writes=TQ, regions=RG)
            A("dve", lambda e, tq=tq: e.tensor_scalar(out=stat[:, 8:9], in0=tq[:, 0:1], scalar1=flb, scalar2=fla, op0=ALU.mult, op1=ALU.add),
              reads=TQ + CONST, writes=["st8"], regions=RG)
            A("dve", lambda e, tq=tq, m2=m2: e.scalar_tensor_tensor(out=tq[:, 0:NM], in0=tq[:, 0:NM], scalar=0.5, in1=m2[:, 0:NM], op0=ALU.mult, op1=ALU.mult),
              reads=TQ + M2 + ["st8"], writes=TQ, regions=RG)
            A("dve", lambda e, tq=tq, m2=m2: e.tensor_tensor(out=tq[:, 0:1], in0=stat[:, 8:9], in1=m2[:, 0:1], op=ALU.mult),
              reads=TQ + M2 + ["st8"], writes=TQ, regions=RG)
            A("dve", lambda e, ab=ab: e.tensor_scalar(out=ab[:, 0:1], in0=ab[:, 0:1], scalar1=cpk[:, C_FLAG + 1:C_FLAG + 2], scalar2=None, op0=ALU.mult),
              reads=AA + CONST + TQ, writes=AA, regions=RG)
            a_s0 = ab[:, 1024:NM].rearrange("p (s t) -> p s t", t=8)[:, :, 0]
            b_s0 = tq[:, 1024:NM].rearrange("p (s t) -> p s t", t=8)[:, :, 0]
            A("dve", lambda e, a_s0=a_s0, h=h: e.tensor_tensor(out=stat[:, 16:32], in0=a_s0, in1=hst[:, h, :], op=ALU.mult),
              reads=AA + ["hst"], writes=["st16"], regions=RG)
            A("dve", lambda e, b_s0=b_s0: e.tensor_tensor(out=b_s0, in0=b_s0, in1=stat[:, 16:32], op=ALU.add),
              reads=TQ + ["st16"], writes=TQ, regions=RG)
            A("dve", lambda e, a_s0=a_s0: e.memset(a_s0, 0.0), reads=["st16"], writes=AA, regions=RG)
            A("dve", lambda e, ab=ab, tq=tq, m2=m2: e.tensor_tensor_scan(out=m2[:, 0:NM], data0=ab[:, 0:NM], data1=tq[:, 0:NM], initial=0.0, op0=ALU.mult, op1=ALU.add),
              reads=AA + TQ + M2, writes=M2, regions=RG)

        def c_tail(h):
            s = h % NS3
            t2 = h % 2
            ab, tq, m2, thb = abuf[s], tqb[s], m2b[s], thbb[t2]
            TQ = [("tq", s)]
            M2 = [("m2", s)]
            AA = [("a", s)]
            A("dve", lambda e, ab=ab, tq=tq: e.tensor_tensor_scan(out=tq[:, 0:1024], data0=ab[:, 0:1024], data1=ab[:, 0:1024], initial=1.0, op0=ALU.mult, op1=ALU.bypass),
              reads=AA + TQ, writes=TQ, regions=RG)
            A("dve", lambda e, m2=m2, h=h, thb=thb: e.tensor_tensor(out=cat[:, h, :], in0=m2[:, 0:NM], in1=thb, op=ALU.mult),
              reads=M2 + [("thb", t2), "constCL"], writes=[("cat", h)], regions=RG + CS + CL)
            A("dve", lambda e, tq=tq, h=h, thb=thb: e.tensor_tensor(out=Qb[:, h, :], in0=tq[:, 0:1024], in1=thb[:, 0:1024], op=ALU.mult),
              reads=TQ + [("thb", t2)], writes=[("Q", h)], regions=RG + CS)
            A("act", lambda e, m2=m2, h=h: e.activation(out=hx[:, h:h + 1], in_=m2[:, 1023:1024], func=AF.Copy), reads=M2, writes=[("hx", h)], regions=RG)
            A("act", lambda e, tq=tq, h=h: e.activation(out=pend[:, h:h + 1], in_=tq[:, 1023:1024], func=AF.Copy), reads=TQ, writes=[("pend", h)], regions=RG)
            A("act", lambda e, m2=m2, h=h: e.activation(out=stt[:, h, 3:4], in_=m2[:, 1023:1024], func=AF.Copy), reads=M2 + ["stt"], writes=[("stt", h)], regions=RG)
            A("act", lambda e, m2=m2, h=h: e.activation(out=stt[:, h, 52:68], in_=m2[:, 1024:NM].rearrange("p (s t) -> p s t", t=8)[:, :, 7], func=AF.Copy),
              reads=M2 + ["stt"], writes=[("stt", h)], regions=RG)

        for i in range(10):
            hx_, hg, hc = i, i - 1, i - 2
            if hx_ < 8:
                x_pe(hx_)
            if i < len(constb_pieces):
                constb_pieces[i]()
            if i == len(constb_pieces):
                S.new_epoch("ZP")
                load_wv()
            if 0 <= hg < 8:
                g_gates(hg)
            if 0 <= hc < 8:
                c_main(hc)
            if 0 <= hg < 8:
                g_m2(hg)
            if hx_ < 8:
                x_evac(hx_)
            if 0 <= hg < 8:
                g_exp(hg)
                c_sqrt(hg)
            if 0 <= hc < 8:
                c_tail(hc)
            if hx_ < 8:
                x_taps(hx_)
            if 0 <= hg < 8:
                g_gr(hg)
            if hx_ < 8:
                x_cast(hx_)

        def start_exchange():
            HXALL = [("hx", h) for h in range(8)]
            A("pool", lambda e: e.dma_start(out=hx_in.ap(), in_=hx[:]), reads=HXALL, writes=["hx_in"], dma="hxo")
            A("pool", lambda e: e.collective_compute("AllGather", ALU.bypass, replica_groups=[[0, 1], [2, 3], [4, 5], [6, 7]],
                                                     ins=[hx_in.ap().opt()], outs=[hx_out.ap().opt()]),
              reads=["hx_in"], writes=["hx_out"], dma="cc", inc=1)


        QALL = [("Q", h) for h in range(8)]

        def fix_begin():
            A("sp", lambda e: e.dma_start(out=hin[:], in_=hx_out.ap()[0:128, :]), reads=["hx_out"], writes=["hin"], dma="hxi")

        def fix_heads(hs):
            for h in hs:
                A("dve", lambda e, h=h: e.scalar_tensor_tensor(out=cat[:, h, 0:1024], in0=Qb[:, h, :], scalar=hin[:, h:h + 1], in1=cat[:, h, 0:1024], op0=ALU.mult, op1=ALU.add),
                  reads=[("Q", h), "hin", ("cat", h)], writes=[("cat", h)], regions=RG)

        def fix_end():
            stt_h = stt[:, :, 3]
            PENDALL = [("pend", h) for h in range(8)]
            A("dve", lambda e: e.tensor_tensor(out=pend[:], in0=pend[:], in1=hin[:], op=ALU.mult), reads=PENDALL + ["hin"], writes=PENDALL)
            A("dve", lambda e: e.tensor_tensor(out=stt_h, in0=stt_h, in1=pend[:], op=ALU.add), reads=PENDALL + [("stt", h) for h in range(8)] + ["stt"], writes=["stt"])
            A("sp", lambda e: e.dma_start(out=st_d, in_=stt[:]), reads=[("stt", h) for h in range(8)] + ["stt"], dma="o_st")
        Wo = R3[:, :].bitcast(BF16).rearrange("p (a b) -> p a b", a=16)
        WOALL = [("Wo", q) for q in range(16)]
        WO_ORDER = [0, 1, 2, 3, 4, 5, 10, 6, 7, 8, 9, 11, 12, 13, 14, 15]
        wo_next = [0]
        wo_limit = [0]

        def load_wo_piece():
            i = wo_next[0]
            if i < min(16, wo_limit[0]):
                wo_next[0] += 1
                q = WO_ORDER[i]
                extra = []
                if 6 <= q <= 9:
                    extra = [("pslab", 0), ("pslab", 1), ("pslab", 2)]
                if q >= 11:
                    extra = list(QALL)
                A("pool", lambda e, q=q: e.dma_start(out=Wo[:, q:q + 1, :], in_=wout_d[:, q:q + 1, :]), writes=[("Wo", q)] + extra, dma="wout", regions=RG)

        S.new_epoch("CS")
        S.new_epoch("VN")
        A("sp", lambda e: e.dma_start(out=sgug, in_=sgug_d), writes=["sgug"], dma="c12", regions=CS)
        S.new_epoch("R3")
        wo_limit[0] = 4
        for ti in range(9):
            v_tile(ti)
            if ti % 2 == 1:
                load_wo_piece()
        wo_limit[0] = 7

        def wo_hook():
            load_wo_piece()
            load_wo_piece()
        after_load[0] = wo_hook

        S.new_epoch("ZP")
        S.new_epoch("CS")
        ub = [zpf32[:, 0:1152], zpf32[:, 1152:2304]]
        gb = [zpf32[:, 2304:3456], zpf32[:, 3456:4608]]
        def sgu_u_pe(h):
            fjob(take_slab(16 + h), BLK_M, [0, 1, 2])

        def sgu_u_chain(h):
            s = h % 2
            PSU = [("ps", b) for b in (0, 1, 2)]
            ups = psflat(0, NM)
            A("act", lambda e, s=s, ups=ups: e.activation(out=ub[s], in_=ups, func=AF.Square), reads=PSU + ["constB"], writes=[("ub", s)], regions=ZP)
            A("dve", lambda e, s=s, ups=ups: e.scalar_tensor_tensor(out=ub[s], in0=ub[s], scalar=1.0 / GC, in1=ups, op0=ALU.add, op1=ALU.mult),
              reads=[("ub", s)] + PSU, writes=[("ub", s)], regions=ZP)
            A("act", lambda e, s=s: e.activation(out=ub[s], in_=ub[s], func=AF.Tanh, scale=GK * GC), reads=[("ub", s)], writes=[("ub", s)], regions=ZP)
            A("dve", lambda e, s=s, ups=ups: e.scalar_tensor_tensor(out=ub[s], in0=ub[s], scalar=1.0, in1=ups, op0=ALU.add, op1=ALU.mult),
              reads=[("ub", s)] + PSU, writes=[("ub", s)] + PSU, regions=ZP)

        def sgu_gs(h):
            s = h % 2
            bkg = [3, 4, 5]
            fjob(take_slab(32 + h), BLK_M, bkg)
            PSG = [("ps", b) for b in bkg]
            gps = psflat(3, NM)
            A("act", lambda e, s=s, gps=gps: e.activation(out=gb[s], in_=gps, func=AF.Silu), reads=PSG + ["constB"], writes=[("gb", s)] + PSG, regions=ZP)
            A("dve", lambda e, s=s: e.tensor_tensor(out=ub[s], in0=ub[s], in1=gb[s], op=ALU.mult), reads=[("ub", s), ("gb", s)], writes=[("ub", s)], regions=ZP)

        def sgu_spatial(h):
            s = h % 2
            for g0, cs in ((0, [0, 1, 2, 3]), (1, [4, 5, 6, 7]), (0, [8])):
                bank = 6 + g0

                def spf(e, cs=cs, bank=bank, h=h):
                    ins = None
                    for j, c in enumerate(cs):
                        o = ps[:, bank, j * 128:(j + 1) * 128]
                        wmat = wsT_b if c < 8 else wsS_b
                        boff = h * 128 if c < 8 else 1024 + h * 128
                        e.matmul(o, lhsT=vn[:, c, h * 128:(h + 1) * 128], rhs=wmat[:, h, :], start=True, stop=False)
                        ins = e.matmul(o, lhsT=onesK[:, :], rhs=biasK[:, boff:boff + 128], start=False, stop=True)
                    return ins
                A("pe", spf, reads=[("vn", c) for c in cs] + [("wsT_b", hh) for hh in range(8)] + [("wsS_b", hh) for hh in range(8)] + BIASC, writes=[("ps", bank)], regions=["VN"])
                n = 128 * len(cs)
                t0 = cs[0] * 128
                A("dve", lambda e, bank=bank, n=n, t0=t0, h=h, s=s: e.scalar_tensor_tensor(out=cat[:, 8 + h, t0:t0 + n], in0=ps[:, bank, 0:n], scalar=0.5,
                                                                                      in1=ub[s][:, t0:t0 + n], op0=ALU.mult, op1=ALU.mult),
                  reads=[("ps", bank), ("ub", s)], writes=[("cat", 8 + h, t0), ("ps", bank)], regions=ZP + CS)

        sgu_u_pe(0)
        sgu_u_chain(0)
        sgu_gs(0)
        start_exchange()
        for h in range(8):
            if h + 1 < 8:
                sgu_u_pe(h + 1)
            sgu_spatial(h)
            if h + 1 < 8:
                sgu_u_chain(h + 1)
                sgu_gs(h + 1)
            if h == 1:
                fix_begin()
            if h == 2:
                wo_limit[0] = 11
            if 1 <= h <= 4:
                fix_heads([2 * (h - 1), 2 * (h - 1) + 1])
            if h == 4:
                fix_end()
            if h == 5:
                wo_limit[0] = 16

        wo_limit[0] = 16
        while wo_next[0] < 16:
            load_wo_piece()
        S.new_epoch("ZP")
        S.new_epoch("ZM")
        xo = [zmf32[:, 0:2048], zmf32[:, 2048:4096]]
        yo = [zmf32[:, 4096:6144], zmf32[:, 6144:8192]]
        postg = zpf32[:, 0:2048]
        ojunk = zpf32[:, 2048:3072].bitcast(BF16)
        A("sp", lambda e: e.dma_start(out=postg, in_=postg_d), writes=["postg"], dma="c13", regions=ZP)
        CATALL = [("cat", k) for k in range(8)] + [("cat", 8 + k, t0) for k in range(8) for t0 in (0, 512, 1024)]
        for ti in range(9):
            s = ti % 2
            tok = ti * 128
            A("sp", lambda e, s=s, ti=ti: e.dma_start(out=xo[s], in_=x_d[ti * 128:(ti + 1) * 128, :]), writes=[("xo", s)], dma=f"xo{s}", regions=ZM)
            bk = [4 * s + q for q in range(4)]

            def of(e, tok=tok, bk=bk):
                ins = None
                for kc in range(16):
                    for q in range(4):
                        ins = e.matmul(ps[:, bk[q], :], lhsT=cat[:, kc, tok:tok + 128], rhs=Wo[:, kc, q * 512:(q + 1) * 512], start=(kc == 0), stop=(kc == 15))
                return ins
            A("pe", of, reads=WOALL + CATALL, writes=[("ps", b) for b in bk], regions=RG)
            PSO = [("ps", b) for b in bk]
            ops_ = psflat(bk[0], 2048)
            A("act", lambda e, ops_=ops_, ti=ti: e.activation(out=ojunk, in_=ops_, func=AF.Square, accum_out=st5[:, 3 * ti:3 * ti + 1]), reads=PSO, writes=["ojunk", ("s5", ti, 0)], regions=ZP)
            A("dve", lambda e, ti=ti: e.tensor_scalar(out=st5[:, 3 * ti + 1:3 * ti + 2], in0=st5[:, 3 * ti:3 * ti + 1], scalar1=1.0 / D, scalar2=EPS, op0=ALU.mult, op1=ALU.add),
              reads=[("s5", ti, 0)], writes=[("s5", ti, 1)])
            A("pool", lambda e, ti=ti: e.tensor_tensor(out=st5[:, 3 * ti + 2:3 * ti + 3], in0=st5[:, 3 * ti + 1:3 * ti + 2], in1=mhalf[:], op=ALU.pow), reads=[("s5", ti, 1), "mhalf"], writes=[("s5", ti, 2)])
            A("dve", lambda e, s=s, ops_=ops_, ti=ti: e.scalar_tensor_tensor(out=yo[s], in0=ops_, scalar=st5[:, 3 * ti + 2:3 * ti + 3], in1=postg, op0=ALU.mult, op1=ALU.mult),
              reads=PSO + [("s5", ti, 2), "postg"], writes=[("yo", s)] + PSO, regions=ZP + ZM)
            if ti < 8:
                A("pool", lambda e, s=s: e.tensor_tensor(out=yo[s], in0=yo[s], in1=xo[s], op=ALU.add), reads=[("yo", s), ("xo", s)], writes=[("yo", s)], regions=ZM)
                A("sp", lambda e, s=s, ti=ti: e.dma_start(out=y_d[ti * 128:(ti + 1) * 128, :], in_=yo[s]), reads=[("yo", s)], dma=f"yo{s}", regions=ZM)
            else:
                for hf in range(2):
                    c0_, c1_ = hf * 1024, (hf + 1) * 1024
                    A("dve", lambda e, s=s, c0_=c0_, c1_=c1_: e.tensor_tensor(out=yo[s][:, c0_:c1_], in0=yo[s][:, c0_:c1_], in1=xo[s][:, c0_:c1_], op=ALU.add),
                      reads=[("yo", s), ("xo", s)], writes=[("yo", s, hf)], regions=ZM)
                    A("sp", lambda e, s=s, ti=ti, c0_=c0_, c1_=c1_: e.dma_start(out=y_d[ti * 128:(ti + 1) * 128, c0_:c1_], in_=yo[s][:, c0_:c1_]),
                      reads=[("yo", s, hf)], dma=f"yl{hf}", regions=ZM)
        A("sp", None, writes=[("yo", 0), ("yo", 1), ("yo", 0, 0), ("yo", 0, 1), ("vb", 0), ("vb", 1), "stt"] + [("stt", h) for h in range(8)])
        S.emit(nc, st)
    return nc


_NC_CACHE = {}


def _prep_inputs(x_prompt, x_sample, state_rglru_conv, state_rglru_h, pre_norm_g, post_norm_g,
                 w_in, conv_w, conv_b, w_rgate, b_rgate, w_igate, b_igate, lru_lambda,
                 sgu_norm_g, w_spatial, b_spatial, w_out):
    f = lambda a: np.ascontiguousarray(np.asarray(a, dtype=np.float32))
    x_prompt, x_sample = f(x_prompt), f(x_sample)
    w = f(w_in)[0]
    w40 = np.ascontiguousarray(w.reshape(16, 128, 40, 128).transpose(2, 1, 0, 3))
    wout = np.ascontiguousarray(f(w_out)[0].reshape(16, 128, D).transpose(1, 0, 2))
    wv = np.ascontiguousarray(w[:, 3072:4096].reshape(16, 128, 1024).transpose(1, 0, 2))
    fm = lambda v: np.ascontiguousarray(f(v).reshape(8, 128).T)
    cw = np.ascontiguousarray(f(conv_w)[0].reshape(4, 8, 128).transpose(2, 1, 0))
    wr = np.ascontiguousarray(f(w_rgate)[0].transpose(1, 0, 2))
    wi = np.ascontiguousarray(f(w_igate)[0].transpose(1, 0, 2))
    ws = f(w_spatial)[0]
    wsT = np.ascontiguousarray(ws.transpose(2, 0, 1))
    wsS = np.zeros((128, 8, 128), np.float32)
    maskS = np.zeros((128, 128), np.float32)
    tri8 = np.triu(np.ones((8, 8), np.float32))
    for n in range(16):
        wsS[n * 8:(n + 1) * 8, :, n * 8:(n + 1) * 8] = ws[:, :8, :8].transpose(2, 0, 1)
        maskS[n * 8:(n + 1) * 8, n * 8:(n + 1) * 8] = tri8
    mask = np.triu(np.ones((128, 128), np.float32))
    idn = np.eye(128, dtype=np.float32)
    bs = f(b_spatial)[0]
    bsp = np.concatenate([bs.reshape(-1), np.tile(bs[:, :8], (1, 16)).reshape(-1)])[None, :]
    sgug = np.ascontiguousarray(np.broadcast_to(f(sgu_norm_g)[0][None, :], (128, 1024)))
    postg = np.ascontiguousarray(np.broadcast_to(f(post_norm_g)[0][None, :], (128, D)))
    pregt = np.ascontiguousarray(np.broadcast_to(f(pre_norm_g)[0][None, :], (128, D)))
    sc = f(state_rglru_conv)[0]
    sh = f(state_rglru_h)[0]
    maps = []
    for c in range(8):
        b, half = c // 2, c % 2
        x = np.zeros((NXR, D), np.float32)
        x[0:1024] = x_prompt[b, half * 1024:(half + 1) * 1024]
        x[1024:1152] = x_sample[16 * c:16 * c + 16].reshape(128, D)
        if half == 1:
            x[1152:1155] = x_prompt[b, 1021:1024]
        cpk = np.zeros((128, 64), np.float32)
        cpk[:, 0:8] = fm(f(conv_b)[0])
        cpk[:, 8:16] = f(b_rgate)[0].T
        cpk[:, 16:24] = f(b_igate)[0].T
        cpk[:, 24:32] = fm(f(lru_lambda)[0])
        cpk[:, 32] = 1.0 if half == 0 else 0.0
        cpk[:, 33] = 0.0 if half == 0 else 1.0
        cst = np.ascontiguousarray(sc[16 * c:16 * c + 16].reshape(16, 3, 8, 128).transpose(3, 2, 0, 1))
        hst = np.ascontiguousarray(sh[16 * c:16 * c + 16].reshape(16, 8, 128).transpose(2, 1, 0))
        maps.append(dict(x=x, w40=w40, wout=wout, wv=wv, cpk=cpk, cw=cw, wr=wr, wi=wi, wsT=wsT, wsS=wsS, mask=mask, maskS=maskS, idn=idn,
                         bsp=np.ascontiguousarray(bsp), sgug=sgug, postg=postg, pregt=pregt, cst=cst, hst=hst))
    return maps


def kernel(**inputs):
    maps = _prep_inputs(**inputs)
    if "nc" not in _NC_CACHE:
        _NC_CACHE["nc"] = build_nc()
    nc = _NC_CACHE["nc"]
    res = run_bass_kernel_spmd(nc, maps, core_ids=list(range(8)))
    R = res.results
    y_prompt = np.zeros((4, 2048, D), np.float32)
    y_sample = np.zeros((128, 8, D), np.float32)
    conv_p = np.zeros((1, 4, 3, 1024), np.float32)
    h_p = np.zeros((1, 4, 1024), np.float32)
    conv_s = np.zeros((1, 128, 3, 1024), np.float32)
    h_s = np.zeros((1, 128, 1024), np.float32)
    v_s = np.zeros((1, 128, 8, 1024), np.float32)
    for c in range(8):
        b, half = c // 2, c % 2
        r = R[c]
        y = np.asarray(r["y"])
        y_prompt[b, half * 1024:(half + 1) * 1024] = y[0:1024]
        y_sample[16 * c:16 * c + 16] = y[1024:1152].reshape(16, 8, D)
        stt = np.asarray(r["st"])
        v_s[0, 16 * c:16 * c + 16] = np.asarray(r["vs"]).reshape(16, 8, 1024)
        cs = stt[:, :, 4:52].reshape(128, 8, 16, 3)
        conv_s[0, 16 * c:16 * c + 16] = cs.transpose(2, 3, 1, 0).reshape(16, 3, 1024)
        h_s[0, 16 * c:16 * c + 16] = stt[:, :, 52:68].transpose(2, 1, 0).reshape(16, 1024)
        if half == 1:
            conv_p[0, b] = stt[:, :, 0:3].transpose(2, 1, 0).reshape(3, 1024)
            h_p[0, b] = stt[:, :, 3].T.reshape(1024)
    return (y_prompt, y_sample, conv_p, h_p, conv_s, h_s, v_s)
```

```python
from contextlib import ExitStack
import numpy as np
import concourse.bass as bass
import concourse.mybir as mybir
from concourse.bass_utils import run_bass_kernel_spmd

F32 = mybir.dt.float32
BF16 = mybir.dt.bfloat16
AF = mybir.ActivationFunctionType
ALU = mybir.AluOpType

ENGS = ("pe", "act", "dve", "pool", "sp")

D = 2048
NMAIN = 1024
NSMP = 128
NM = NMAIN + NSMP
NZ = NM + 4
NXR = 1280
EPS = 1e-6
GK = 0.7978845608028654
GC = 0.044715


class Sched:
    def __init__(self):
        self.ops = []
        self.last_w = {}
        self.readers = {}
        self.dma_count = {}
        self.reg_cur = {}
        self.reg_prev = {}

    def new_epoch(self, region):
        self.reg_prev[region] = self.reg_prev.get(region, set()) | self.reg_cur.get(region, set())
        self.reg_cur[region] = set()

    def add(self, eng, fn, reads=(), writes=(), dma=None, regions=(), inc=16):
        idx = len(self.ops)
        deps = set()
        for r in reads:
            if r in self.last_w:
                deps.add(self.last_w[r])
        for w in writes:
            if w in self.last_w:
                deps.add(self.last_w[w])
            for rd in self.readers.get(w, ()):
                deps.add(rd)
        for rg in regions:
            deps |= self.reg_prev.get(rg, set())
            self.reg_cur.setdefault(rg, set()).add(idx)
        deps.discard(idx)
        op = dict(eng=eng, fn=fn, deps=deps, dma=dma, idx=idx, ms=None)
        if dma is not None:
            self.dma_count[dma] = self.dma_count.get(dma, 0) + inc
            op["dma_ord"] = self.dma_count[dma]
            op["inc"] = inc
        self.ops.append(op)
        for r in reads:
            self.readers.setdefault(r, []).append(idx)
        for w in writes:
            self.last_w[w] = idx
            self.readers[w] = []
        return idx

    def emit(self, nc, stack):
        ops = self.ops
        need = set()
        for op in ops:
            for d in op["deps"]:
                a = ops[d]
                if a["dma"] is None:
                    if a["eng"] == "pe" and op["eng"] == "pe" and op["dma"] is None:
                        continue
                    need.add(d)
        cnt = {e: 0 for e in ENGS}
        for op in ops:
            if op["dma"] is None and op["idx"] in need:
                cnt[op["eng"]] += 1
                op["ms"] = cnt[op["eng"]]
        sem_eng = {e: stack.enter_context(nc.semaphore("s_" + e)) for e in ENGS if cnt[e] > 0}
        sem_dma = {k: stack.enter_context(nc.semaphore("d_" + str(k))) for k in self.dma_count}
        block = stack.enter_context(nc.Block())

        def run(eng_name, eng):
            waited = {}
            for op in ops:
                if op["eng"] != eng_name:
                    continue
                wl = {}
                for d in op["deps"]:
                    a = ops[d]
                    if a["dma"] is not None:
                        key = ("d", a["dma"])
                        val = a["dma_ord"]
                    else:
                        if a["eng"] == "pe" and eng_name == "pe" and op["dma"] is None:
                            continue
                        key = ("e", a["eng"])
                        val = a["ms"]
                    if wl.get(key, 0) < val:
                        wl[key] = val
                for key, val in wl.items():
                    if waited.get(key, 0) >= val:
                        continue
                    waited[key] = val
                    sem = sem_dma[key[1]] if key[0] == "d" else sem_eng[key[1]]
                    eng.wait_ge(sem, val)
                if op["fn"] is None:
                    continue
                ins = op["fn"](eng)
                if op["dma"] is not None:
                    ins.then_inc(sem_dma[op["dma"]], op["inc"])
                elif op["ms"] is not None:
                    ins.then_inc(sem_eng[eng_name], 1)

        @block.tensor
        def _(e):
            run("pe", e)

        @block.scalar
        def _(e):
            run("act", e)

        @block.vector
        def _(e):
            run("dve", e)

        @block.gpsimd
        def _(e):
            run("pool", e)

        @block.sync
        def _(e):
            run("sp", e)


def build_nc():
    nc = bass.Bass("TRN2", target_bir_lowering=False)

    def din(name, shape):
        return nc.dram_tensor(name, list(shape), F32, kind="ExternalInput").ap()

    def dout(name, shape):
        return nc.dram_tensor(name, list(shape), F32, kind="ExternalOutput").ap()

    x_d = din("x", [NXR, D])
    w40_d = din("w40", [40, 128, 16, 128])
    wout_d = din("wout", [128, 16, D])
    wv_d = din("wv", [128, 16, 1024])
    cpk_d = din("cpk", [128, 64])
    cw_d = din("cw", [128, 8, 4])
    wr_d = din("wr", [128, 8, 128])
    wi_d = din("wi", [128, 8, 128])
    wsT_d = din("wsT", [128, 8, 128])
    wsS_d = din("wsS", [128, 8, 128])
    mask_d = din("mask", [128, 128])
    maskS_d = din("maskS", [128, 128])
    idn_d = din("idn", [128, 128])
    bsp_d = din("bsp", [1, 2048])
    sgug_d = din("sgug", [128, 1024])
    postg_d = din("postg", [128, D])
    pregt_d = din("pregt", [128, D])
    cst_d = din("cst", [128, 8, 16, 3])
    hst_d = din("hst", [128, 8, 16])
    y_d = dout("y", [NM, D])
    vs_d = dout("vs", [128, 1024])
    st_d = dout("st", [128, 8, 68])
    hx_in = nc.dram_tensor("hx_in", [128, 8], F32)
    hx_out = nc.dram_tensor("hx_out", [256, 8], F32)

    S = Sched()
    A = S.add
    with ExitStack() as st:
        def sb(name, shape, dt):
            return st.enter_context(nc.sbuf_tensor("sb_" + name, list(shape), dt))

        zTp = sb("zTp", [128, 16, 1024], BF16)
        zTm = sb("zTm", [128, 16, NZ], BF16)
        cat = sb("cat", [128, 16, NM], BF16)
        R3 = sb("R3", [128, 16384], F32)
        vn = sb("vn", [128, 9, 1024], BF16)
        cpk = sb("cpk", [128, 64], F32)
        cw = sb("cw", [128, 8, 4], F32)
        der = sb("der", [128, 64], F32)
        wr_b = sb("wr_b", [128, 8, 128], BF16)
        wi_b = sb("wi_b", [128, 8, 128], BF16)
        wsT_b = sb("wsT_b", [128, 8, 128], BF16)
        wsS_b = sb("wsS_b", [128, 8, 128], BF16)
        idb = sb("idb", [128, 128], BF16)
        biasK = sb("biasK", [128, 2048], BF16)
        onesK = sb("onesK", [128, 128], BF16)
        stt = sb("stt", [128, 8, 68], F32)
        cst = sb("cst", [128, 8, 16, 3], F32)
        hst = sb("hst", [128, 8, 16], F32)
        stat = sb("stat", [128, 64], F32)
        mhalf = sb("mhalf", [128, 1], F32)
        st1 = sb("st1", [128, 3 * 10], F32)
        st3 = sb("st3", [128, 4 * 9], F32)
        st5 = sb("st5", [128, 3 * 9], F32)
        hx = sb("hx", [128, 8], F32)
        pend = sb("pend", [128, 8], F32)
        hin = sb("hin", [128, 8], F32)
        ps = st.enter_context(nc.psum_tensor("ps", [128, 8, 512], F32))

        def psflat(b0, n):
            return ps[:, b0:b0 + (n + 511) // 512, :].rearrange("p a b -> p (a b)")[:, 0:n]

        zpf = zTp[:, :, :].rearrange("p a b -> p (a b)")
        zpf32 = zpf.bitcast(F32)
        zmf32 = zTm[:, :, :].rearrange("p a b -> p (a b)").bitcast(F32)
        vnf = vn[:, :, :].rearrange("p a b -> p (a b)")
        catf = cat[:, 8:16, :].rearrange("p a b -> p (a b)").bitcast(F32)

        C_CB, C_BR, C_BI, C_LAM, C_FLAG = 0, 8, 16, 24, 32
        D_HBR, D_HBI, D_HS8, D_QS8, D_FLA, D_FLB, D_T0, D_T1, D_T2 = 0, 8, 16, 24, 32, 33, 34, 42, 50

        RG = ["R3"]
        CS = ["CS"]
        ZM = ["ZM"]
        CL = ["CL"]
        NXS = 4
        xt = [R3[:, i * 2048:(i + 1) * 2048] for i in range(NXS)]
        ztb = [R3[:, 8192 + i * 1024:8192 + (i + 1) * 1024].bitcast(BF16) for i in range(3)]
        pregt = R3[:, 11264:13312]
        s_idn = catf[:, 0:128]
        vb = [catf[:, 0:1024], catf[:, 1024:2048]]
        sgug = catf[:, 2048:3072]
        vjunk = catf[:, 3072:3584].bitcast(BF16)
        sqj = catf[:, 3584:4608].bitcast(BF16)
        Wv = zTp
        WVALL = [("Wv", j) for j in range(4)]
        A("sp", lambda e: e.dma_start(out=s_idn, in_=idn_d), writes=["s_idn"], dma="c9", regions=CS)
        A("sp", lambda e: e.dma_start(out=xt[0], in_=x_d[0:128, :]), writes=["xt0"], dma="xt0", regions=RG)
        A("sp", lambda e: e.dma_start(out=pregt, in_=pregt_d), writes=["pregt"], dma="c11", regions=RG)
        for t in range(1, NXS):
            A("sp", lambda e, t=t: e.dma_start(out=xt[t], in_=x_d[t * 128:(t + 1) * 128, :]), writes=[f"xt{t}"], dma=f"xt{t}", regions=RG)

        def load_wv():
            for j in range(4):
                A("pool", lambda e, j=j: e.dma_start(out=Wv[:, 4 * j:4 * j + 4, :], in_=wv_d[:, 4 * j:4 * j + 4, :]), writes=[("Wv", j)], dma="wv", regions=["ZP"])
        A("dve", lambda e: e.memset(mhalf[:], -0.5), writes=["mhalf"])
        A("dve", lambda e: e.tensor_copy(out=idb[:], in_=s_idn), reads=["s_idn"], writes=["idb"], regions=CS)

        def zT_tokens(lo, hi):
            toks = []
            for t in range(lo // 128, (hi + 127) // 128):
                toks += [("zT", t, 0), ("zT", t, 1)]
            return toks

        def p1_a(t):
            sl = t % NXS
            z2 = t % 3
            c0 = 3 * t
            if t >= NXS:
                nrow = 128 if t < 9 else 3
                A("sp", lambda e, t=t, sl=sl, nrow=nrow: e.dma_start(out=xt[sl][0:nrow, :], in_=x_d[t * 128:t * 128 + nrow, :]), writes=[f"xt{sl}"], dma=f"xt{sl}", regions=RG)
            A("act", lambda e, sl=sl, c0=c0: e.activation(out=sqj, in_=xt[sl], func=AF.Square, accum_out=st1[:, c0:c0 + 1]),
              reads=[f"xt{sl}"], writes=["sqj", ("s1", t, 0)], regions=RG + CS)
            A("dve", lambda e, c0=c0: e.tensor_scalar(out=st1[:, c0 + 1:c0 + 2], in0=st1[:, c0:c0 + 1], scalar1=1.0 / D, scalar2=EPS, op0=ALU.mult, op1=ALU.add),
              reads=[("s1", t, 0)], writes=[("s1", t, 1)])
            A("pool", lambda e, c0=c0: e.tensor_tensor(out=st1[:, c0 + 2:c0 + 3], in0=st1[:, c0 + 1:c0 + 2], in1=mhalf[:], op=ALU.pow),
              reads=[("s1", t, 1), "mhalf"], writes=[("s1", t, 2)])

        def p1_a2(t):
            sl = t % NXS
            z2 = t % 3
            c0 = 3 * t
            A("dve", lambda e, sl=sl, z2=z2, c0=c0: e.scalar_tensor_tensor(out=ztb[z2], in0=xt[sl], scalar=st1[:, c0 + 2:c0 + 3], in1=pregt, op0=ALU.mult, op1=ALU.mult),
              reads=[f"xt{sl}", ("s1", t, 2), "pregt"], writes=[f"zt{z2}"], regions=RG)

        def p1_b(t):
            z2 = t % 3
            pb = 2 * (t % 2)
            psb = ps[:, pb:pb + 2, :].rearrange("p a b -> p (a b)").bitcast(BF16)

            def trf(e, z2=z2, psb=psb):
                ins = None
                for kc in range(16):
                    ins = e.transpose(out=psb[:, kc * 128:(kc + 1) * 128], in_=ztb[z2][:, kc * 128:(kc + 1) * 128], identity=idb[:])
                return ins
            A("pe", trf, reads=[f"zt{z2}", "idb"], writes=[("ps", pb), ("ps", pb + 1)], regions=RG)
            zo = t * 128
            nn = 128 if t < 9 else 3
            A("act", lambda e, zo=zo, nn=nn, psb=psb: e.activation(out=zTm[:, 0:8, zo:zo + nn], in_=psb[:, 0:1024].rearrange("p (a b) -> p a b", a=8)[:, :, 0:nn], func=AF.Copy),
              reads=[("ps", pb)], writes=[("zT", t, 0)], regions=ZM)
            A("dve", lambda e, zo=zo, nn=nn, psb=psb: e.tensor_copy(out=zTm[:, 8:16, zo:zo + nn], in_=psb[:, 1024:2048].rearrange("p (a b) -> p a b", a=8)[:, :, 0:nn]),
              reads=[("ps", pb + 1)], writes=[("zT", t, 1)], regions=ZM)

        def v_tile(ti):
            tok = ti * 128
            s = ti % 2
            bk = [2 * ti, 2 * ti + 1] if ti < 2 else [4 + 2 * s, 5 + 2 * s]

            def vf(e, tok=tok, bk=bk):
                ins = None
                for kc in range(16):
                    for half in range(2):
                        ins = e.matmul(ps[:, bk[half], :], lhsT=zTm[:, kc, tok:tok + 128], rhs=Wv[:, kc, half * 512:(half + 1) * 512],
                                       start=(kc == 0), stop=(kc == 15))
                return ins
            A("pe", vf, reads=WVALL + zT_tokens(tok, tok + 128), writes=[("ps", b) for b in bk], regions=["ZP", "ZM"])
            PSV = [("ps", b) for b in bk]
            vps = psflat(bk[0], 1024)
            A("act", lambda e, s=s, vps=vps: e.activation(out=vb[s], in_=vps, func=AF.Gelu_apprx_tanh), reads=PSV, writes=[("vb", s)] + PSV, regions=CS)
            c0 = 4 * ti
            A("act", lambda e, s=s, c0=c0: e.activation(out=vjunk, in_=vb[s], func=AF.Square, accum_out=st3[:, c0:c0 + 1]), reads=[("vb", s)], writes=["vjunk", ("s3", ti, 0)], regions=CS)
            A("dve", lambda e, c0=c0: e.tensor_scalar(out=st3[:, c0 + 1:c0 + 2], in0=st3[:, c0:c0 + 1], scalar1=1.0 / 1024, scalar2=EPS, op0=ALU.mult, op1=ALU.add),
              reads=[("s3", ti, 0)], writes=[("s3", ti, 1)])
            A("pool", lambda e, c0=c0: e.tensor_tensor(out=st3[:, c0 + 2:c0 + 3], in0=st3[:, c0 + 1:c0 + 2], in1=mhalf[:], op=ALU.pow), reads=[("s3", ti, 1), "mhalf"], writes=[("s3", ti, 2)])
            A("dve", lambda e, c0=c0: e.tensor_scalar(out=st3[:, c0 + 3:c0 + 4], in0=st3[:, c0 + 2:c0 + 3], scalar1=1.0, scalar2=None, op0=ALU.mult), reads=[("s3", ti, 2)], writes=[("s3", ti, 3)])
            rs = st3[:, c0 + 3:c0 + 4]
            if ti < 8:
                A("dve", lambda e, s=s, ti=ti, rs=rs: e.scalar_tensor_tensor(out=vn[:, ti, :], in0=vb[s], scalar=rs, in1=sgug, op0=ALU.mult, op1=ALU.mult),
                  reads=[("vb", s), ("s3", ti, 3), "sgug"], writes=[("vn", ti)], regions=CS + ["VN"])
            else:
                A("dve", lambda e, s=s, rs=rs: e.scalar_tensor_tensor(out=vb[s], in0=vb[s], scalar=rs, in1=sgug, op0=ALU.mult, op1=ALU.mult),
                  reads=[("vb", s), ("s3", ti, 3), "sgug"], writes=[("vb", s)], regions=CS)
                A("sp", lambda e, s=s: e.dma_start(out=vs_d, in_=vb[s]), reads=[("vb", s)], dma="o_vs", regions=CS)
                A("act", lambda e, s=s, ti=ti: e.activation(out=vn[:, ti, :], in_=vb[s], func=AF.Copy), reads=[("vb", s)], writes=[("vn", ti)], regions=CS + ["VN"])

        def const_a():
            A("act", lambda e: e.dma_start(out=cpk[:], in_=cpk_d), writes=["cpk"], dma="c0")
            A("act", lambda e: e.dma_start(out=cw[:], in_=cw_d), writes=["cw"], dma="c1")
            A("act", lambda e: e.dma_start(out=cst[:], in_=cst_d), writes=["cst"], dma="c2")
            A("act", lambda e: e.dma_start(out=hst[:], in_=hst_d), writes=["hst"], dma="c3")
            A("dve", lambda e: e.memset(stt[:], 0.0), writes=["stt"])
            T0 = der[:, D_T0:D_T0 + 8]
            T1 = der[:, D_T1:D_T1 + 8]
            T2 = der[:, D_T2:D_T2 + 8]
            lam = cpk[:, C_LAM:C_LAM + 8]
            HS8 = der[:, D_HS8:D_HS8 + 8]
            QS8 = der[:, D_QS8:D_QS8 + 8]
            A("dve", lambda e: e.tensor_scalar(out=T0, in0=lam, scalar1=-1.0, scalar2=None, op0=ALU.mult), reads=["cpk"], writes=["T0"])
            A("dve", lambda e: e.tensor_tensor(out=T1, in0=T0, in1=lam, op=ALU.min), reads=["T0", "cpk"], writes=["T1"])
            A("act", lambda e: e.activation(out=T1, in_=T1, func=AF.Exp), reads=["T1"], writes=["T1"])
            A("dve", lambda e: e.tensor_scalar(out=T2, in0=T1, scalar1=1.0, scalar2=None, op0=ALU.add), reads=["T1"], writes=["T2"])
            A("act", lambda e: e.activation(out=HS8, in_=T2, func=AF.Ln), reads=["T2"], writes=["HS8"])
            A("dve", lambda e: e.tensor_scalar(out=T2, in0=T2, scalar1=-1.0, scalar2=1e-30, op0=ALU.add, op1=ALU.max), reads=["T2", "HS8"], writes=["T2"])
            A("dve", lambda e: e.reciprocal(out=T2, in_=T2), reads=["T2"], writes=["T2"])
            A("dve", lambda e: e.tensor_tensor(out=T1, in0=T1, in1=T2, op=ALU.mult), reads=["T1", "T2"], writes=["T1"])
            A("dve", lambda e: e.tensor_tensor(out=T1, in0=T1, in1=HS8, op=ALU.mult), reads=["T1", "HS8"], writes=["T1"])
            A("dve", lambda e: e.tensor_scalar(out=T0, in0=T0, scalar1=0.0, scalar2=None, op0=ALU.max), reads=["T0"], writes=["T0"])
            A("dve", lambda e: e.tensor_tensor(out=T1, in0=T1, in1=T0, op=ALU.add), reads=["T1", "T0"], writes=["T1"])
            A("dve", lambda e: e.tensor_scalar(out=HS8, in0=T1, scalar1=-4.0, scalar2=None, op0=ALU.mult), reads=["T1"], writes=["HS8"])
            A("dve", lambda e: e.tensor_scalar(out=QS8, in0=T1, scalar1=2.0, scalar2=None, op0=ALU.mult), reads=["T1"], writes=["QS8"])
            A("dve", lambda e: e.tensor_scalar(out=der[:, D_HBR:D_HBR + 16], in0=cpk[:, C_BR:C_BR + 16], scalar1=0.5, scalar2=None, op0=ALU.mult),
              reads=["cpk"], writes=["HB"])
            A("dve", lambda e: e.tensor_scalar(out=der[:, D_FLA:D_FLA + 2], in0=cpk[:, C_FLAG:C_FLAG + 2], scalar1=0.5, scalar2=None, op0=ALU.mult),
              reads=["cpk"], writes=["FL"])
            return ["HS8", "QS8", "HB", "FL", "cpk", "cw"]

        def const_b():
            catl = cat[:, 0:8, :].rearrange("p a b -> p (a b)").bitcast(F32)
            ZP_ = ["ZP"]
            s_wr = zpf32[:, 0:1024].rearrange("p (a b) -> p a b", a=8)
            s_wi = zpf32[:, 1024:2048].rearrange("p (a b) -> p a b", a=8)
            s_wsT = zpf32[:, 2048:3072].rearrange("p (a b) -> p a b", a=8)
            s_wsS = zpf32[:, 3072:4096].rearrange("p (a b) -> p a b", a=8)
            s_mask = zpf32[:, 4096:4224]
            s_maskS = zpf32[:, 7296:7424]
            s_b = zpf32[0:1, 4224:6272]
            s_h = zpf32[0:1, 6272:7296].bitcast(BF16)
            s_l = zpf32[0:1, 0:1024].bitcast(BF16)
            A("sp", lambda e: e.dma_start(out=s_wr, in_=wr_d), writes=["s_wr"], dma="c4", regions=ZP_)
            A("sp", lambda e: e.dma_start(out=s_wi, in_=wi_d), writes=["s_wi"], dma="c5", regions=ZP_)
            A("sp", lambda e: e.dma_start(out=s_wsT, in_=wsT_d), writes=["s_wsT"], dma="c6", regions=ZP_)
            A("sp", lambda e: e.dma_start(out=s_wsS, in_=wsS_d), writes=["s_wsS"], dma="c7", regions=ZP_)
            A("sp", lambda e: e.dma_start(out=s_mask, in_=mask_d), writes=["s_mask"], dma="c8", regions=ZP_)
            A("sp", lambda e: e.dma_start(out=s_maskS, in_=maskS_d), writes=["s_maskS"], dma="c8s", regions=ZP_)
            A("sp", lambda e: e.dma_start(out=s_b, in_=bsp_d), writes=["s_b"], dma="c10", regions=ZP_)

            def p0():
                A("act", lambda e: e.activation(out=wr_b[:], in_=s_wr, func=AF.Copy), reads=["s_wr"], writes=["wr_b"], regions=ZP_)
                A("act", lambda e: e.activation(out=wi_b[:], in_=s_wi, func=AF.Copy), reads=["s_wi"], writes=["wi_b"], regions=ZP_)

            def p1():
                A("dve", lambda e: e.memset(biasK[:], 0.0), writes=["biasK"])
                A("dve", lambda e: e.memset(onesK[:], 0.0), writes=["onesK"])
                A("dve", lambda e: e.memset(onesK[0:2, :], 1.0), writes=["onesK"])

            def p1b():
                A("act", lambda e: e.activation(out=s_h, in_=s_b, func=AF.Copy), reads=["s_b"], writes=["s_h"], regions=ZP_)

            def p2():
                A("dve", lambda e: e.tensor_tensor(out=s_b, in0=s_b, in1=s_h, op=ALU.subtract), reads=["s_b", "s_h"], writes=["s_b"], regions=ZP_)

            def p2b():
                A("act", lambda e: e.activation(out=s_l, in_=s_b, func=AF.Copy), reads=["s_b"], writes=["s_l", "s_wr"], regions=ZP_)
                A("sp", lambda e: e.dma_start(out=biasK[0:1, :], in_=s_h), reads=["s_h", "biasK"], writes=["biasK", "constCL"], dma="c14", regions=ZP_)
                A("sp", lambda e: e.dma_start(out=biasK[1:2, :], in_=s_l), reads=["s_l", "biasK"], writes=["biasK", "constCL"], dma="c15", regions=ZP_)

            def p3():
                for hh in range(4):
                    A("dve", lambda e, hh=hh: e.tensor_tensor(out=wsT_b[:, hh, :], in0=s_wsT[:, hh, :], in1=s_mask, op=ALU.mult),
                      reads=["s_wsT", "s_mask"], writes=[("wsT_b", hh)], regions=ZP_)

            def p3b():
                for hh in range(4, 8):
                    A("dve", lambda e, hh=hh: e.tensor_tensor(out=wsT_b[:, hh, :], in0=s_wsT[:, hh, :], in1=s_mask, op=ALU.mult),
                      reads=["s_wsT", "s_mask"], writes=[("wsT_b", hh)], regions=ZP_)
                for hh in range(8):
                    A("dve", lambda e, hh=hh: e.tensor_tensor(out=wsS_b[:, hh, :], in0=s_wsS[:, hh, :], in1=s_maskS, op=ALU.mult),
                      reads=["s_wsS", "s_maskS"], writes=[("wsS_b", hh)] + (["constB"] if hh == 7 else []), regions=ZP_)
            return [p0, p3, p3b, p1, p1b, p2, p2b]


        CONST = const_a()
        for t in range(12):
            if t < 10:
                p1_a(t)
            if 0 <= t - 2 < 10:
                p1_b(t - 2)
            if t < 10:
                p1_a2(t)
            if t in (6, 9):
                for k, chunk in enumerate((0, 1, 8)):
                    if (k == 0) != (t == 6):
                        continue
                    dst = R3[:, 13312 + 1024 * k:13312 + 1024 * (k + 1)].bitcast(BF16).rearrange("p (a b) -> p a b", a=16)
                    A("pool", lambda e, chunk=chunk, dst=dst: e.dma_start(out=dst, in_=w40_d[chunk]), writes=[("eslab", k)], dma="sl_eslab_%d" % k)
        RG = ["R3"]

        slab_ctr = [0]

        SET2 = [("a", 2), ("tq", 2), ("m2", 2), ("xc", 2)]

        def slab_view(k):
            if k < 3:
                return R3[:, 13312 + 1024 * k:13312 + 1024 * (k + 1)].bitcast(BF16), [], ("eslab", k), []
            if k < 16:
                sl = k % 3
                return vnf[:, sl * 2048:(sl + 1) * 2048], ["VN"], ("slab", sl), []
            if k < 19:
                j = k - 16
                return R3[:, 6912 + 1024 * j:6912 + 1024 * (j + 1)].bitcast(BF16), ["R3"], ("pslab", j), SET2
            sl = k % 3
            return slab4[:, sl * 2048:(sl + 1) * 2048], ["ZP"], ("slab4", sl), []

        def load_slab_k(k, chunk):
            view, reg, tok, extra = slab_view(k)
            dst = view.rearrange("p (a b) -> p a b", a=16)
            A("pool", lambda e, chunk=chunk, dst=dst: e.dma_start(out=dst, in_=w40_d[chunk]), writes=[tok] + extra, dma="sl_%s_%s" % tok, regions=reg)

        def fjob(k, blocks, banks, split=False):
            view, reg, tok, _ = slab_view(k)
            groups = [[(bl, b)] for bl, b in zip(blocks, banks)] if split else [list(zip(blocks, banks))]
            for grp in groups:
                def f(e, grp=grp):
                    ins = None
                    for kc in range(16):
                        for (lo, n), b in grp:
                            ins = e.matmul(ps[:, b, 0:n], lhsT=view[:, kc * 128:(kc + 1) * 128], rhs=zTm[:, kc, lo:lo + n],
                                           start=(kc == 0), stop=(kc == 15))
                    return ins
                rd = [tok]
                for (lo, n), b in grp:
                    rd += zT_tokens(lo, lo + n)
                A("pe", f, reads=rd, writes=[("ps", b) for _, b in grp], regions=list(reg) + ZM)

        slab_seq = [0]
        for h in range(1, 8):
            slab_seq += [h, 8 + h - 1]
        slab_seq += [15]
        for h in range(8):
            slab_seq += [16 + h, 32 + h]
        sq_loaded = [3]
        sq_taken = [0]
        after_load = [None]

        def take_slab(chunk):
            k = sq_taken[0]
            assert slab_seq[k] == chunk, (k, chunk, slab_seq[k])
            while sq_loaded[0] < len(slab_seq) and sq_loaded[0] <= k + 2:
                load_slab_k(sq_loaded[0], slab_seq[sq_loaded[0]])
                sq_loaded[0] += 1
                if after_load[0] is not None:
                    after_load[0]()
            sq_taken[0] += 1
            return k

        BLK_M = [(0, 512), (512, 512), (1024, 128)]
        BLK_X = [(0, 512), (512, 512), (1024, 131)]

        S.new_epoch("R3")
        S.new_epoch("CL")
        S.new_epoch("CS")
        ZP = ["ZP"]
        slab4 = zpf[:, 9216:9216 + 6144]
        BIASC = ["biasK", "onesK"]
        constb_pieces = const_b()
        NS3 = 3
        abuf = [R3[:, s * 3456:s * 3456 + 1152] for s in range(NS3)]
        tqb = [R3[:, s * 3456 + 1152:s * 3456 + 2304] for s in range(NS3)]
        m2b = [R3[:, s * 3456 + 2304:s * 3456 + 3456] for s in range(NS3)]
        xcbb = [R3[:, 10368 + s * 576:10368 + (s + 1) * 576].bitcast(BF16) for s in range(NS3)]
        Qb = R3[:, 12096:12096 + 4096].bitcast(BF16).rearrange("p (a b) -> p a b", a=8)
        thbb = [catf[:, 0:1152], catf[:, 1152:2304]]
        xr = catf[:, 2304:2304 + 1203]
        xr_s = xr[:, 1027:1203].rearrange("p (s r) -> p s r", r=11)

        def x_pe(h):
            fjob(take_slab(h), BLK_X, [0, 1, 2], split=(h == 0))

        def x_evac(h):
            s = h % NS3
            ab = abuf[s]
            w3 = cw[:, h, 3:4]
            cb = cpk[:, C_CB + h:C_CB + h + 1]
            XC = [("xc", s)]
            PX = [("ps", 0), ("ps", 1), ("ps", 2)]
            A("act", lambda e, h=h: e.activation(out=xr_s[:, :, 0:3], in_=cst[:, h, :, :], func=AF.Copy), reads=["cst"], writes=["xr"], regions=CS)
            A("act", lambda e: e.activation(out=xr[:, 3:1027], in_=psflat(0, 1024), func=AF.Copy), reads=PX, writes=["xr"], regions=CS)
            A("act", lambda e: e.activation(out=xr_s[:, :, 3:11], in_=ps[:, 2, 0:128].rearrange("p (s t) -> p s t", t=8), func=AF.Copy),
              reads=PX, writes=["xr"], regions=CS)
            A("act", lambda e: e.activation(out=xr[:, 0:3], in_=ps[:, 2, 128:131], func=AF.Copy), reads=PX, writes=["xr"], regions=CS)
            A("act", lambda e, ab=ab, w3=w3, cb=cb: e.activation(out=ab[:, 0:NM], in_=psflat(0, NM), func=AF.Identity, scale=w3, bias=cb),
              reads=PX + CONST, writes=XC + [("a", s)], regions=RG)

        def x_taps(h):
            s = h % NS3
            ab = abuf[s]
            XC = [("xc", s)]
            xc_s = ab[:, 1024:NM].rearrange("p (s t) -> p s t", t=8)
            for k in range(3):
                A("dve", lambda e, k=k, h=h, ab=ab: e.scalar_tensor_tensor(out=ab[:, 0:1024], in0=xr[:, k:k + 1024], scalar=cw[:, h, k:k + 1], in1=ab[:, 0:1024],
                                                                        op0=ALU.mult, op1=ALU.add), reads=["xr"] + XC + CONST, writes=XC, regions=RG + CS)
                A("dve", lambda e, k=k, h=h, xc_s=xc_s: e.scalar_tensor_tensor(out=xc_s, in0=xr_s[:, :, k:k + 8], scalar=cw[:, h, k:k + 1], in1=xc_s,
                                                                            op0=ALU.mult, op1=ALU.add), reads=["xr"] + XC + CONST, writes=XC, regions=RG + CS)

        def x_cast(h):
            s = h % NS3
            ab, xcb = abuf[s], xcbb[s]
            XC = [("xc", s)]
            A("act", lambda e, ab=ab, xcb=xcb: e.activation(out=xcb[:, 0:NM], in_=ab[:, 0:NM], func=AF.Copy), reads=XC, writes=[("xcb", s)], regions=RG)
            A("act", lambda e, h=h: e.activation(out=stt[:, h, 0:3], in_=xr[:, 1024:1027], func=AF.Copy), reads=["xr", "stt"], writes=[("stt", h)], regions=CS)
            A("act", lambda e, h=h: e.activation(out=stt[:, h, 4:52].rearrange("p (s r) -> p s r", r=3), in_=xr_s[:, :, 8:11], func=AF.Copy),
              reads=["xr", "stt"], writes=[("stt", h)], regions=CS)

        GBLK = [(0, 512), (512, 512), (1024, 128)]

        def g_gates(h):
            s = h % NS3
            tq, m2, xcb = tqb[s], m2b[s], xcbb[s]
            hbr = der[:, D_HBR + h:D_HBR + h + 1]
            hbi = der[:, D_HBI + h:D_HBI + h + 1]
            j = 0
            for (lo, n) in GBLK:
                for (wmat, wname, dst, dname, bias) in ((wr_b, "wr_b", tq, "tqp", hbr), (wi_b, "wi_b", m2, "m2p", hbi)):
                    bank = 3 + (j % 2)
                    j += 1
                    A("pe", lambda e, lo=lo, n=n, bank=bank, wmat=wmat, xcb=xcb, h=h: e.matmul(ps[:, bank, 0:n], lhsT=wmat[:, h, :], rhs=xcb[:, lo:lo + n], start=True, stop=True),
                      reads=[("xcb", s), wname], writes=[("ps", bank)], regions=RG)
                    A("act", lambda e, lo=lo, n=n, bank=bank, dst=dst, bias=bias: e.activation(out=dst[:, lo:lo + n], in_=ps[:, bank, 0:n], func=AF.Tanh, scale=0.5, bias=bias),
                      reads=[("ps", bank)] + CONST, writes=[(dname, s, lo)] + ([("tq", s) if dname == "tqp" else ("m2", s)] if lo == 0 else []), regions=RG)

        def g_m2(h):
            s = h % NS3
            ab, m2 = abuf[s], m2b[s]
            XC = [("xc", s)]
            M2P = [("m2p", s, lo) for lo, _ in GBLK]
            A("dve", lambda e, m2=m2, ab=ab: e.scalar_tensor_tensor(out=m2[:, 0:NM], in0=m2[:, 0:NM], scalar=1.0, in1=ab[:, 0:NM], op0=ALU.add, op1=ALU.mult),
              reads=M2P + XC, writes=[("m2", s)], regions=RG)

        def g_exp(h):
            s = h % NS3
            ab, tq = abuf[s], tqb[s]
            hs8 = der[:, D_HS8 + h:D_HS8 + h + 1]
            qs8 = der[:, D_QS8 + h:D_QS8 + h + 1]
            XC = [("xc", s)]
            TQP = [("tqp", s, lo) for lo, _ in GBLK]
            A("act", lambda e, ab=ab, tq=tq, hs8=hs8: e.activation(out=ab[:, 0:NM], in_=tq[:, 0:NM], func=AF.Exp, scale=hs8, bias=hs8),
              reads=TQP + [("m2", s)] + CONST, writes=XC + [("a", s)], regions=RG)
            A("act", lambda e, tq=tq, qs8=qs8: e.activation(out=tq[:, 0:NM], in_=tq[:, 0:NM], func=AF.Tanh, scale=qs8, bias=qs8),
              reads=TQP + CONST, writes=[("tq", s)] + TQP, regions=RG)

        def g_gr(h):
            t2 = h % 2
            thb = thbb[t2]
            bkg = [5, 6, 7]
            fjob(take_slab(8 + h), BLK_M, bkg)
            A("act", lambda e, thb=thb: e.activation(out=thb, in_=psflat(5, NM), func=AF.Silu),
              reads=[("ps", b) for b in bkg], writes=[("thb", t2)], regions=CS)

        def c_sqrt(h):
            s = h % NS3
            tq = tqb[s]
            A("act", lambda e, tq=tq: e.activation(out=tq[:, 0:NM], in_=tq[:, 0:NM], func=AF.Sqrt), reads=[("tq", s)], writes=[("tq", s)], regions=RG)

        def c_main(h):
            s = h % NS3
            ab, tq, m2 = abuf[s], tqb[s], m2b[s]
            TQ = [("tq", s)]
            M2 = [("m2", s)]
            AA = [("a", s)]
            fla = der[:, D_FLA:D_FLA + 1]
            flb = der[:, D_FLB:D_FLB + 1]
            A("dve", lambda e, ab=ab, tq=tq: e.scalar_tensor_tensor(out=tq[:, 0:NM], in0=ab[:, 0:NM], scalar=1.0, in1=tq[:, 0:NM], op0=ALU.add, op1=ALU.mult),
              reads=TQ + AA, writes=TQ, regions=RG)
            A("dve", lambda e, tq=tq: e.tensor_scalar(out=stat[:, 8:9], in0=tq[:, 0:1], scalar1=flb, scalar2=fla, op0=ALU.mult, op1=ALU.add),
              reads=TQ + CONST, writes=["st8"], regions=RG)
            A("dve", lambda e, tq=tq, m2=m2: e.scalar_tensor_tensor(out=tq[:, 0:NM], in0=tq[:, 0:NM], scalar=0.5, in1=m2[:, 0:NM], op0=ALU.mult, op1=ALU.mult),
              reads=TQ + M2 + ["st8"], writes=TQ, regions=RG)
            A("dve", lambda e, tq=tq, m2=m2: e.tensor_tensor(out=tq[:, 0:1], in0=stat[:, 8:9], in1=m2[:, 0:1], op=ALU.mult),
              reads=TQ + M2 + ["st8"], writes=TQ, regions=RG)
            A("dve", lambda e, ab=ab: e.tensor_scalar(out=ab[:, 0:1], in0=ab[:, 0:1], scalar1=cpk[:, C_FLAG + 1:C_FLAG + 2], scalar2=None, op0=ALU.mult),
              reads=AA + CONST + TQ, writes=AA, regions=RG)
            a_s0 = ab[:, 1024:NM].rearrange("p (s t) -> p s t", t=8)[:, :, 0]
            b_s0 = tq[:, 1024:NM].rearrange("p (s t) -> p s t", t=8)[:, :, 0]
            A("dve", lambda e, a_s0=a_s0, h=h: e.tensor_tensor(out=stat[:, 16:32], in0=a_s0, in1=hst[:, h, :], op=ALU.mult),
              reads=AA + ["hst"], writes=["st16"], regions=RG)
            A("dve", lambda e, b_s0=b_s0: e.tensor_tensor(out=b_s0, in0=b_s0, in1=stat[:, 16:32], op=ALU.add),
              reads=TQ + ["st16"], writes=TQ, regions=RG)
            A("dve", lambda e, a_s0=a_s0: e.memset(a_s0, 0.0), reads=["st16"], writes=AA, regions=RG)
            A("dve", lambda e, ab=ab, tq=tq, m2=m2: e.tensor_tensor_scan(out=m2[:, 0:NM], data0=ab[:, 0:NM], data1=tq[:, 0:NM], initial=0.0, op0=ALU.mult, op1=ALU.add),
              reads=AA + TQ + M2, writes=M2, regions=RG)

        def c_tail(h):
            s = h % NS3
            t2 = h % 2
            ab, tq, m2, thb = abuf[s], tqb[s], m2b[s], thbb[t2]
            TQ = [("tq", s)]
            M2 = [("m2", s)]
            AA = [("a", s)]
            A("dve", lambda e, ab=ab, tq=tq: e.tensor_tensor_scan(out=tq[:, 0:1024], data0=ab[:, 0:1024], data1=ab[:, 0:1024], initial=1.0, op0=ALU.mult, op1=ALU.bypass),
              reads=AA + TQ, writes=TQ, regions=RG)
            A("dve", lambda e, m2=m2, h=h, thb=thb: e.tensor_tensor(out=cat[:, h, :], in0=m2[:, 0:NM], in1=thb, op=ALU.mult),
              reads=M2 + [("thb", t2), "constCL"], writes=[("cat", h)], regions=RG + CS + CL)
            A("dve", lambda e, tq=tq, h=h, thb=thb: e.tensor_tensor(out=Qb[:, h, :], in0=tq[:, 0:1024], in1=thb[:, 0:1024], op=ALU.mult),
              reads=TQ + [("thb", t2)], writes=[("Q", h)], regions=RG + CS)
            A("act", lambda e, m2=m2, h=h: e.activation(out=hx[:, h:h + 1], in_=m2[:, 1023:1024], func=AF.Copy), reads=M2, writes=[("hx", h)], regions=RG)
            A("act", lambda e, tq=tq, h=h: e.activation(out=pend[:, h:h + 1], in_=tq[:, 1023:1024], func=AF.Copy), reads=TQ, writes=[("pend", h)], regions=RG)
            A("act", lambda e, m2=m2, h=h: e.activation(out=stt[:, h, 3:4], in_=m2[:, 1023:1024], func=AF.Copy), reads=M2 + ["stt"], writes=[("stt", h)], regions=RG)
            A("act", lambda e, m2=m2, h=h: e.activation(out=stt[:, h, 52:68], in_=m2[:, 1024:NM].rearrange("p (s t) -> p s t", t=8)[:, :, 7], func=AF.Copy),
              reads=M2 + ["stt"], writes=[("stt", h)], regions=RG)

        for i in range(10):
            hx_, hg, hc = i, i - 1, i - 2
            if hx_ < 8:
                x_pe(hx_)
            if i < len(constb_pieces):
                constb_pieces[i]()
            if i == len(constb_pieces):
                S.new_epoch("ZP")
                load_wv()
            if 0 <= hg < 8:
                g_gates(hg)
            if 0 <= hc < 8:
                c_main(hc)
            if 0 <= hg < 8:
                g_m2(hg)
            if hx_ < 8:
                x_evac(hx_)
            if 0 <= hg < 8:
                g_exp(hg)
                c_sqrt(hg)
            if 0 <= hc < 8:
                c_tail(hc)
            if hx_ < 8:
                x_taps(hx_)
            if 0 <= hg < 8:
                g_gr(hg)
            if hx_ < 8:
                x_cast(hx_)

        def start_exchange():
            HXALL = [("hx", h) for h in range(8)]
            A("pool", lambda e: e.dma_start(out=hx_in.ap(), in_=hx[:]), reads=HXALL, writes=["hx_in"], dma="hxo")
            A("pool", lambda e: e.collective_compute("AllGather", ALU.bypass, replica_groups=[[0, 1], [2, 3], [4, 5], [6, 7]],
                                                     ins=[hx_in.ap().opt()], outs=[hx_out.ap().opt()]),
              reads=["hx_in"], writes=["hx_out"], dma="cc", inc=1)


        QALL = [("Q", h) for h in range(8)]

        def fix_begin():
            A("sp", lambda e: e.dma_start(out=hin[:], in_=hx_out.ap()[0:128, :]), reads=["hx_out"], writes=["hin"], dma="hxi")

        def fix_heads(hs):
            for h in hs:
                A("dve", lambda e, h=h: e.scalar_tensor_tensor(out=cat[:, h, 0:1024], in0=Qb[:, h, :], scalar=hin[:, h:h + 1], in1=cat[:, h, 0:1024], op0=ALU.mult, op1=ALU.add),
                  reads=[("Q", h), "hin", ("cat", h)], writes=[("cat", h)], regions=RG)

        def fix_end():
            stt_h = stt[:, :, 3]
            PENDALL = [("pend", h) for h in range(8)]
            A("dve", lambda e: e.tensor_tensor(out=pend[:], in0=pend[:], in1=hin[:], op=ALU.mult), reads=PENDALL + ["hin"], writes=PENDALL)
            A("dve", lambda e: e.tensor_tensor(out=stt_h, in0=stt_h, in1=pend[:], op=ALU.add), reads=PENDALL + [("stt", h) for h in range(8)] + ["stt"], writes=["stt"])
            A("sp", lambda e: e.dma_start(out=st_d, in_=stt[:]), reads=[("stt", h) for h in range(8)] + ["stt"], dma="o_st")
        Wo = R3[:, :].bitcast(BF16).rearrange("p (a b) -> p a b", a=16)
        WOALL = [("Wo", q) for q in range(16)]
        WO_ORDER = [0, 1, 2, 3, 4, 5, 10, 6, 7, 8, 9, 11, 12, 13, 14, 15]
        wo_next = [0]
        wo_limit = [0]

        def load_wo_piece():
            i = wo_next[0]
            if i < min(16, wo_limit[0]):
                wo_next[0] += 1
                q = WO_ORDER[i]
                extra = []
                if 6 <= q <= 9:
                    extra = [("pslab", 0), ("pslab", 1), ("pslab", 2)]
                if q >= 11:
                    extra = list(QALL)
                A("pool", lambda e, q=q: e.dma_start(out=Wo[:, q:q + 1, :], in_=wout_d[:, q:q + 1, :]), writes=[("Wo", q)] + extra, dma="wout", regions=RG)

        S.new_epoch("CS")
        S.new_epoch("VN")
        A("sp", lambda e: e.dma_start(out=sgug, in_=sgug_d), writes=["sgug"], dma="c12", regions=CS)
        S.new_epoch("R3")
        wo_limit[0] = 4
        for ti in range(9):
            v_tile(ti)
            if ti % 2 == 1:
                load_wo_piece()
        wo_limit[0] = 7

        def wo_hook():
            load_wo_piece()
            load_wo_piece()
        after_load[0] = wo_hook

        S.new_epoch("ZP")
        S.new_epoch("CS")
        ub = [zpf32[:, 0:1152], zpf32[:, 1152:2304]]
        gb = [zpf32[:, 2304:3456], zpf32[:, 3456:4608]]
        def sgu_u_pe(h):
            fjob(take_slab(16 + h), BLK_M, [0, 1, 2])

        def sgu_u_chain(h):
            s = h % 2
            PSU = [("ps", b) for b in (0, 1, 2)]
            ups = psflat(0, NM)
            A("act", lambda e, s=s, ups=ups: e.activation(out=ub[s], in_=ups, func=AF.Gelu_apprx_tanh), reads=PSU + ["constB"], writes=[("ub", s)] + PSU, regions=ZP)

        def sgu_gs(h):
            s = h % 2
            bkg = [3, 4, 5]
            fjob(take_slab(32 + h), BLK_M, bkg)
            PSG = [("ps", b) for b in bkg]
            gps = psflat(3, NM)
            A("act", lambda e, s=s, gps=gps: e.activation(out=gb[s], in_=gps, func=AF.Silu), reads=PSG + ["constB"], writes=[("gb", s)] + PSG, regions=ZP)
            A("dve", lambda e, s=s: e.tensor_tensor(out=ub[s], in0=ub[s], in1=gb[s], op=ALU.mult), reads=[("ub", s), ("gb", s)], writes=[("ub", s)], regions=ZP)

        def sgu_spatial(h):
            s = h % 2
            for g0, cs in ((0, [0, 1, 2, 3]), (1, [4, 5, 6, 7]), (0, [8])):
                bank = 6 + g0

                def spf(e, cs=cs, bank=bank, h=h):
                    ins = None
                    for j, c in enumerate(cs):
                        o = ps[:, bank, j * 128:(j + 1) * 128]
                        wmat = wsT_b if c < 8 else wsS_b
                        boff = h * 128 if c < 8 else 1024 + h * 128
                        e.matmul(o, lhsT=vn[:, c, h * 128:(h + 1) * 128], rhs=wmat[:, h, :], start=True, stop=False)
                        ins = e.matmul(o, lhsT=onesK[:, :], rhs=biasK[:, boff:boff + 128], start=False, stop=True)
                    return ins
                A("pe", spf, reads=[("vn", c) for c in cs] + [("wsT_b", hh) for hh in range(8)] + [("wsS_b", hh) for hh in range(8)] + BIASC, writes=[("ps", bank)], regions=["VN"])
                n = 128 * len(cs)
                t0 = cs[0] * 128
                A("dve", lambda e, bank=bank, n=n, t0=t0, h=h, s=s: e.scalar_tensor_tensor(out=cat[:, 8 + h, t0:t0 + n], in0=ps[:, bank, 0:n], scalar=1.0,
                                                                                      in1=ub[s][:, t0:t0 + n], op0=ALU.mult, op1=ALU.mult),
                  reads=[("ps", bank), ("ub", s)], writes=[("cat", 8 + h, t0), ("ps", bank)], regions=ZP + CS)

        sgu_u_pe(0)
        sgu_u_chain(0)
        sgu_gs(0)
        start_exchange()
        for h in range(8):
            if h + 1 < 8:
                sgu_u_pe(h + 1)
            sgu_spatial(h)
            if h + 1 < 8:
                sgu_u_chain(h + 1)
                sgu_gs(h + 1)
            if h == 1:
                fix_begin()
            if h == 2:
                wo_limit[0] = 11
            if 1 <= h <= 4:
                fix_heads([2 * (h - 1), 2 * (h - 1) + 1])
            if h == 4:
                fix_end()
            if h == 5:
                wo_limit[0] = 16

        wo_limit[0] = 16
        while wo_next[0] < 16:
            load_wo_piece()
        S.new_epoch("ZP")
        S.new_epoch("ZM")
        xo = [zmf32[:, 0:2048], zmf32[:, 2048:4096]]
        yo = [zmf32[:, 4096:6144], zmf32[:, 6144:8192]]
        postg = zpf32[:, 0:2048]
        ojunk = zpf32[:, 2048:3072].bitcast(BF16)
        A("sp", lambda e: e.dma_start(out=postg, in_=postg_d), writes=["postg"], dma="c13", regions=ZP)
        CATALL = [("cat", k) for k in range(8)] + [("cat", 8 + k, t0) for k in range(8) for t0 in (0, 512, 1024)]
        for ti in range(9):
            s = ti % 2
            tok = ti * 128
            A("sp", lambda e, s=s, ti=ti: e.dma_start(out=xo[s], in_=x_d[ti * 128:(ti + 1) * 128, :]), writes=[("xo", s)], dma=f"xo{s}", regions=ZM)
            bk = [4 * s + q for q in range(4)]

            def of(e, tok=tok, bk=bk):
                ins = None
                for kc in range(16):
                    for q in range(4):
                        ins = e.matmul(ps[:, bk[q], :], lhsT=cat[:, kc, tok:tok + 128], rhs=Wo[:, kc, q * 512:(q + 1) * 512], start=(kc == 0), stop=(kc == 15))
                return ins
            A("pe", of, reads=WOALL + CATALL, writes=[("ps", b) for b in bk], regions=RG)
            PSO = [("ps", b) for b in bk]
            ops_ = psflat(bk[0], 2048)
            A("act", lambda e, ops_=ops_, ti=ti: e.activation(out=ojunk, in_=ops_, func=AF.Square, accum_out=st5[:, 3 * ti:3 * ti + 1]), reads=PSO, writes=["ojunk", ("s5", ti, 0)], regions=ZP)
            A("dve", lambda e, ti=ti: e.tensor_scalar(out=st5[:, 3 * ti + 1:3 * ti + 2], in0=st5[:, 3 * ti:3 * ti + 1], scalar1=1.0 / D, scalar2=EPS, op0=ALU.mult, op1=ALU.add),
              reads=[("s5", ti, 0)], writes=[("s5", ti, 1)])
            A("pool", lambda e, ti=ti: e.tensor_tensor(out=st5[:, 3 * ti + 2:3 * ti + 3], in0=st5[:, 3 * ti + 1:3 * ti + 2], in1=mhalf[:], op=ALU.pow), reads=[("s5", ti, 1), "mhalf"], writes=[("s5", ti, 2)])
            A("dve", lambda e, s=s, ops_=ops_, ti=ti: e.scalar_tensor_tensor(out=yo[s], in0=ops_, scalar=st5[:, 3 * ti + 2:3 * ti + 3], in1=postg, op0=ALU.mult, op1=ALU.mult),
              reads=PSO + [("s5", ti, 2), "postg"], writes=[("yo", s)] + PSO, regions=ZP + ZM)
            if ti < 8:
                A("pool", lambda e, s=s: e.tensor_tensor(out=yo[s], in0=yo[s], in1=xo[s], op=ALU.add), reads=[("yo", s), ("xo", s)], writes=[("yo", s)], regions=ZM)
                A("sp", lambda e, s=s, ti=ti: e.dma_start(out=y_d[ti * 128:(ti + 1) * 128, :], in_=yo[s]), reads=[("yo", s)], dma=f"yo{s}", regions=ZM)
            else:
                for hf in range(2):
                    c0_, c1_ = hf * 1024, (hf + 1) * 1024
                    A("dve", lambda e, s=s, c0_=c0_, c1_=c1_: e.tensor_tensor(out=yo[s][:, c0_:c1_], in0=yo[s][:, c0_:c1_], in1=xo[s][:, c0_:c1_], op=ALU.add),
                      reads=[("yo", s), ("xo", s)], writes=[("yo", s, hf)], regions=ZM)
                    A("sp", lambda e, s=s, ti=ti, c0_=c0_, c1_=c1_: e.dma_start(out=y_d[ti * 128:(ti + 1) * 128, c0_:c1_], in_=yo[s][:, c0_:c1_]),
                      reads=[("yo", s, hf)], dma=f"yl{hf}", regions=ZM)
        A("sp", None, writes=[("yo", 0), ("yo", 1), ("yo", 0, 0), ("yo", 0, 1), ("vb", 0), ("vb", 1), "stt"] + [("stt", h) for h in range(8)])
        S.emit(nc, st)
    return nc


_NC_CACHE = {}


def _prep_inputs(x_prompt, x_sample, state_rglru_conv, state_rglru_h, pre_norm_g, post_norm_g,
                 w_in, conv_w, conv_b, w_rgate, b_rgate, w_igate, b_igate, lru_lambda,
                 sgu_norm_g, w_spatial, b_spatial, w_out):
    f = lambda a: np.ascontiguousarray(np.asarray(a, dtype=np.float32))
    x_prompt, x_sample = f(x_prompt), f(x_sample)
    w = f(w_in)[0]
    w40 = np.ascontiguousarray(w.reshape(16, 128, 40, 128).transpose(2, 1, 0, 3))
    wout = np.ascontiguousarray(f(w_out)[0].reshape(16, 128, D).transpose(1, 0, 2))
    wv = np.ascontiguousarray(w[:, 3072:4096].reshape(16, 128, 1024).transpose(1, 0, 2))
    fm = lambda v: np.ascontiguousarray(f(v).reshape(8, 128).T)
    cw = np.ascontiguousarray(f(conv_w)[0].reshape(4, 8, 128).transpose(2, 1, 0))
    wr = np.ascontiguousarray(f(w_rgate)[0].transpose(1, 0, 2))
    wi = np.ascontiguousarray(f(w_igate)[0].transpose(1, 0, 2))
    ws = f(w_spatial)[0]
    wsT = np.ascontiguousarray(ws.transpose(2, 0, 1))
    wsS = np.zeros((128, 8, 128), np.float32)
    maskS = np.zeros((128, 128), np.float32)
    tri8 = np.triu(np.ones((8, 8), np.float32))
    for n in range(16):
        wsS[n * 8:(n + 1) * 8, :, n * 8:(n + 1) * 8] = ws[:, :8, :8].transpose(2, 0, 1)
        maskS[n * 8:(n + 1) * 8, n * 8:(n + 1) * 8] = tri8
    mask = np.triu(np.ones((128, 128), np.float32))
    idn = np.eye(128, dtype=np.float32)
    bs = f(b_spatial)[0]
    bsp = np.concatenate([bs.reshape(-1), np.tile(bs[:, :8], (1, 16)).reshape(-1)])[None, :]
    sgug = np.ascontiguousarray(np.broadcast_to(f(sgu_norm_g)[0][None, :], (128, 1024)))
    postg = np.ascontiguousarray(np.broadcast_to(f(post_norm_g)[0][None, :], (128, D)))
    pregt = np.ascontiguousarray(np.broadcast_to(f(pre_norm_g)[0][None, :], (128, D)))
    sc = f(state_rglru_conv)[0]
    sh = f(state_rglru_h)[0]
    maps = []
    for c in range(8):
        b, half = c // 2, c % 2
        x = np.zeros((NXR, D), np.float32)
        x[0:1024] = x_prompt[b, half * 1024:(half + 1) * 1024]
        x[1024:1152] = x_sample[16 * c:16 * c + 16].reshape(128, D)
        if half == 1:
            x[1152:1155] = x_prompt[b, 1021:1024]
        cpk = np.zeros((128, 64), np.float32)
        cpk[:, 0:8] = fm(f(conv_b)[0])
        cpk[:, 8:16] = f(b_rgate)[0].T
        cpk[:, 16:24] = f(b_igate)[0].T
        cpk[:, 24:32] = fm(f(lru_lambda)[0])
        cpk[:, 32] = 1.0 if half == 0 else 0.0
        cpk[:, 33] = 0.0 if half == 0 else 1.0
        cst = np.ascontiguousarray(sc[16 * c:16 * c + 16].reshape(16, 3, 8, 128).transpose(3, 2, 0, 1))
        hst = np.ascontiguousarray(sh[16 * c:16 * c + 16].reshape(16, 8, 128).transpose(2, 1, 0))
        maps.append(dict(x=x, w40=w40, wout=wout, wv=wv, cpk=cpk, cw=cw, wr=wr, wi=wi, wsT=wsT, wsS=wsS, mask=mask, maskS=maskS, idn=idn,
                         bsp=np.ascontiguousarray(bsp), sgug=sgug, postg=postg, pregt=pregt, cst=cst, hst=hst))
    return maps


def kernel(**inputs):
    maps = _prep_inputs(**inputs)
    if "nc" not in _NC_CACHE:
        _NC_CACHE["nc"] = build_nc()
    nc = _NC_CACHE["nc"]
    res = run_bass_kernel_spmd(nc, maps, core_ids=list(range(8)))
    R = res.results
    y_prompt = np.zeros((4, 2048, D), np.float32)
    y_sample = np.zeros((128, 8, D), np.float32)
    conv_p = np.zeros((1, 4, 3, 1024), np.float32)
    h_p = np.zeros((1, 4, 1024), np.float32)
    conv_s = np.zeros((1, 128, 3, 1024), np.float32)
    h_s = np.zeros((1, 128, 1024), np.float32)
    v_s = np.zeros((1, 128, 8, 1024), np.float32)
    for c in range(8):
        b, half = c // 2, c % 2
        r = R[c]
        y = np.asarray(r["y"])
        y_prompt[b, half * 1024:(half + 1) * 1024] = y[0:1024]
        y_sample[16 * c:16 * c + 16] = y[1024:1152].reshape(16, 8, D)
        stt = np.asarray(r["st"])
        v_s[0, 16 * c:16 * c + 16] = np.asarray(r["vs"]).reshape(16, 8, 1024)
        cs = stt[:, :, 4:52].reshape(128, 8, 16, 3)
        conv_s[0, 16 * c:16 * c + 16] = cs.transpose(2, 3, 1, 0).reshape(16, 3, 1024)
        h_s[0, 16 * c:16 * c + 16] = stt[:, :, 52:68].transpose(2, 1, 0).reshape(16, 1024)
        if half == 1:
            conv_p[0, b] = stt[:, :, 0:3].transpose(2, 1, 0).reshape(3, 1024)
            h_p[0, b] = stt[:, :, 3].T.reshape(1024)
    return (y_prompt, y_sample, conv_p, h_p, conv_s, h_s, v_s)
```

```python
from contextlib import ExitStack
import numpy as np
import concourse.bass as bass
import concourse.mybir as mybir
from concourse.bass_utils import run_bass_kernel_spmd

F32 = mybir.dt.float32
BF16 = mybir.dt.bfloat16
AF = mybir.ActivationFunctionType
ALU = mybir.AluOpType

ENGS = ("pe", "act", "dve", "pool", "sp")

D = 2048
NMAIN = 1024
NSMP = 128
NM = NMAIN + NSMP
NZ = NM + 4
NXR = 1280
EPS = 1e-6
GK = 0.7978845608028654
GC = 0.044715


class Sched:
    def __init__(self):
        self.ops = []
        self.last_w = {}
        self.readers = {}
        self.dma_count = {}
        self.reg_cur = {}
        self.reg_prev = {}

    def new_epoch(self, region):
        self.reg_prev[region] = self.reg_prev.get(region, set()) | self.reg_cur.get(region, set())
        self.reg_cur[region] = set()

    def add(self, eng, fn, reads=(), writes=(), dma=None, regions=(), inc=16):
        idx = len(self.ops)
        deps = set()
        for r in reads:
            if r in self.last_w:
                deps.add(self.last_w[r])
        for w in writes:
            if w in self.last_w:
                deps.add(self.last_w[w])
            for rd in self.readers.get(w, ()):
                deps.add(rd)
        for rg in regions:
            deps |= self.reg_prev.get(rg, set())
            self.reg_cur.setdefault(rg, set()).add(idx)
        deps.discard(idx)
        op = dict(eng=eng, fn=fn, deps=deps, dma=dma, idx=idx, ms=None)
        if dma is not None:
            self.dma_count[dma] = self.dma_count.get(dma, 0) + inc
            op["dma_ord"] = self.dma_count[dma]
            op["inc"] = inc
        self.ops.append(op)
        for r in reads:
            self.readers.setdefault(r, []).append(idx)
        for w in writes:
            self.last_w[w] = idx
            self.readers[w] = []
        return idx

    def emit(self, nc, stack):
        ops = self.ops
        need = set()
        for op in ops:
            for d in op["deps"]:
                a = ops[d]
                if a["dma"] is None:
                    if a["eng"] == "pe" and op["eng"] == "pe" and op["dma"] is None:
                        continue
                    need.add(d)
        cnt = {e: 0 for e in ENGS}
        for op in ops:
            if op["dma"] is None and op["idx"] in need:
                cnt[op["eng"]] += 1
                op["ms"] = cnt[op["eng"]]
        sem_eng = {e: stack.enter_context(nc.semaphore("s_" + e)) for e in ENGS if cnt[e] > 0}
        sem_dma = {k: stack.enter_context(nc.semaphore("d_" + str(k))) for k in self.dma_count}
        block = stack.enter_context(nc.Block())

        def run(eng_name, eng):
            waited = {}
            for op in ops:
                if op["eng"] != eng_name:
                    continue
                wl = {}
                for d in op["deps"]:
                    a = ops[d]
                    if a["dma"] is not None:
                        key = ("d", a["dma"])
                        val = a["dma_ord"]
                    else:
                        if a["eng"] == "pe" and eng_name == "pe" and op["dma"] is None:
                            continue
                        key = ("e", a["eng"])
                        val = a["ms"]
                    if wl.get(key, 0) < val:
                        wl[key] = val
                for key, val in wl.items():
                    if waited.get(key, 0) >= val:
                        continue
                    waited[key] = val
                    sem = sem_dma[key[1]] if key[0] == "d" else sem_eng[key[1]]
                    eng.wait_ge(sem, val)
                if op["fn"] is None:
                    continue
                ins = op["fn"](eng)
                if op["dma"] is not None:
                    ins.then_inc(sem_dma[op["dma"]], op["inc"])
                elif op["ms"] is not None:
                    ins.then_inc(sem_eng[eng_name], 1)

        @block.tensor
        def _(e):
            run("pe", e)

        @block.scalar
        def _(e):
            run("act", e)

        @block.vector
        def _(e):
            run("dve", e)

        @block.gpsimd
        def _(e):
            run("pool", e)

        @block.sync
        def _(e):
            run("sp", e)


def build_nc():
    nc = bass.Bass("TRN2", target_bir_lowering=False)

    def din(name, shape):
        return nc.dram_tensor(name, list(shape), F32, kind="ExternalInput").ap()

    def dout(name, shape):
        return nc.dram_tensor(name, list(shape), F32, kind="ExternalOutput").ap()

    x_d = din("x", [NXR, D])
    w40_d = din("w40", [40, 128, 16, 128])
    wout_d = din("wout", [128, 16, D])
    wv_d = din("wv", [128, 16, 1024])
    cpk_d = din("cpk", [128, 64])
    cw_d = din("cw", [128, 8, 4])
    wr_d = din("wr", [128, 8, 128])
    wi_d = din("wi", [128, 8, 128])
    wsT_d = din("wsT", [128, 8, 128])
    wsS_d = din("wsS", [128, 8, 128])
    mask_d = din("mask", [128, 128])
    maskS_d = din("maskS", [128, 128])
    idn_d = din("idn", [128, 128])
    bsp_d = din("bsp", [1, 2048])
    sgug_d = din("sgug", [128, 1024])
    postg_d = din("postg", [128, D])
    pregt_d = din("pregt", [128, D])
    cst_d = din("cst", [128, 8, 16, 3])
    hst_d = din("hst", [128, 8, 16])
    y_d = dout("y", [NM, D])
    vs_d = dout("vs", [128, 1024])
    st_d = dout("st", [128, 8, 68])
    hx_in = nc.dram_tensor("hx_in", [128, 8], F32)
    hx_out = nc.dram_tensor("hx_out", [256, 8], F32)

    S = Sched()
    A = S.add
    with ExitStack() as st:
        def sb(name, shape, dt):
            return st.enter_context(nc.sbuf_tensor("sb_" + name, list(shape), dt))

        zTp = sb("zTp", [128, 16, 1024], BF16)
        zTm = sb("zTm", [128, 16, NZ], BF16)
        cat = sb("cat", [128, 16, NM], BF16)
        R3 = sb("R3", [128, 16384], F32)
        vn = sb("vn", [128, 9, 1024], BF16)
        cpk = sb("cpk", [128, 64], F32)
        cw = sb("cw", [128, 8, 4], F32)
        der = sb("der", [128, 64], F32)
        wr_b = sb("wr_b", [128, 8, 128], BF16)
        wi_b = sb("wi_b", [128, 8, 128], BF16)
        wsT_b = sb("wsT_b", [128, 8, 128], BF16)
        wsS_b = sb("wsS_b", [128, 8, 128], BF16)
        idb = sb("idb", [128, 128], BF16)
        biasK = sb("biasK", [128, 2048], BF16)
        onesK = sb("onesK", [128, 128], BF16)
        stt = sb("stt", [128, 8, 68], F32)
        cst = sb("cst", [128, 8, 16, 3], F32)
        hst = sb("hst", [128, 8, 16], F32)
        stat = sb("stat", [128, 64], F32)
        mhalf = sb("mhalf", [128, 1], F32)
        st1 = sb("st1", [128, 3 * 10], F32)
        st3 = sb("st3", [128, 4 * 9], F32)
        st5 = sb("st5", [128, 3 * 9], F32)
        hx = sb("hx", [128, 8], F32)
        pend = sb("pend", [128, 8], F32)
        hin = sb("hin", [128, 8], F32)
        ps = st.enter_context(nc.psum_tensor("ps", [128, 8, 512], F32))

        def psflat(b0, n):
            return ps[:, b0:b0 + (n + 511) // 512, :].rearrange("p a b -> p (a b)")[:, 0:n]

        zpf = zTp[:, :, :].rearrange("p a b -> p (a b)")
        zpf32 = zpf.bitcast(F32)
        zmf32 = zTm[:, :, :].rearrange("p a b -> p (a b)").bitcast(F32)
        vnf = vn[:, :, :].rearrange("p a b -> p (a b)")
        catf = cat[:, 8:16, :].rearrange("p a b -> p (a b)").bitcast(F32)

        C_CB, C_BR, C_BI, C_LAM, C_FLAG = 0, 8, 16, 24, 32
        D_HBR, D_HBI, D_HS8, D_QS8, D_FLA, D_FLB, D_T0, D_T1, D_T2 = 0, 8, 16, 24, 32, 33, 34, 42, 50

        RG = ["R3"]
        CS = ["CS"]
        ZM = ["ZM"]
        CL = ["CL"]
        NXS = 4
        xt = [R3[:, i * 2048:(i + 1) * 2048] for i in range(NXS)]
        ztb = [R3[:, 8192 + i * 1024:8192 + (i + 1) * 1024].bitcast(BF16) for i in range(3)]
        pregt = R3[:, 11264:13312]
        s_idn = catf[:, 0:128]
        vb = [catf[:, 0:1024], catf[:, 1024:2048]]
        sgug = catf[:, 2048:3072]
        vjunk = catf[:, 3072:3584].bitcast(BF16)
        sqj = catf[:, 3584:4608].bitcast(BF16)
        Wv = zTp
        WVALL = [("Wv", j) for j in range(4)]
        A("sp", lambda e: e.dma_start(out=s_idn, in_=idn_d), writes=["s_idn"], dma="c9", regions=CS)
        A("sp", lambda e: e.dma_start(out=xt[0], in_=x_d[0:128, :]), writes=["xt0"], dma="xt0", regions=RG)
        A("sp", lambda e: e.dma_start(out=pregt, in_=pregt_d), writes=["pregt"], dma="c11", regions=RG)
        for t in range(1, NXS):
            A("sp", lambda e, t=t: e.dma_start(out=xt[t], in_=x_d[t * 128:(t + 1) * 128, :]), writes=[f"xt{t}"], dma=f"xt{t}", regions=RG)

        def load_wv():
            for j in range(4):
                A("pool", lambda e, j=j: e.dma_start(out=Wv[:, 4 * j:4 * j + 4, :], in_=wv_d[:, 4 * j:4 * j + 4, :]), writes=[("Wv", j)], dma="wv", regions=["ZP"])
        A("dve", lambda e: e.memset(mhalf[:], -0.5), writes=["mhalf"])
        A("dve", lambda e: e.tensor_copy(out=idb[:], in_=s_idn), reads=["s_idn"], writes=["idb"], regions=CS)

        def zT_tokens(lo, hi):
            toks = []
            for t in range(lo // 128, (hi + 127) // 128):
                toks += [("zT", t, 0), ("zT", t, 1)]
            return toks

        def p1_a(t):
            sl = t % NXS
            z2 = t % 3
            c0 = 3 * t
            if t >= NXS:
                nrow = 128 if t < 9 else 3
                A("sp", lambda e, t=t, sl=sl, nrow=nrow: e.dma_start(out=xt[sl][0:nrow, :], in_=x_d[t * 128:t * 128 + nrow, :]), writes=[f"xt{sl}"], dma=f"xt{sl}", regions=RG)
            A("act", lambda e, sl=sl, c0=c0: e.activation(out=sqj, in_=xt[sl], func=AF.Square, accum_out=st1[:, c0:c0 + 1]),
              reads=[f"xt{sl}"], writes=["sqj", ("s1", t, 0)], regions=RG + CS)
            A("dve", lambda e, c0=c0: e.tensor_scalar(out=st1[:, c0 + 1:c0 + 2], in0=st1[:, c0:c0 + 1], scalar1=1.0 / D, scalar2=EPS, op0=ALU.mult, op1=ALU.add),
              reads=[("s1", t, 0)], writes=[("s1", t, 1)])
            A("pool", lambda e, c0=c0: e.tensor_tensor(out=st1[:, c0 + 2:c0 + 3], in0=st1[:, c0 + 1:c0 + 2], in1=mhalf[:], op=ALU.pow),
              reads=[("s1", t, 1), "mhalf"], writes=[("s1", t, 2)])

        def p1_a2(t):
            sl = t % NXS
            z2 = t % 3
            c0 = 3 * t
            A("dve", lambda e, sl=sl, z2=z2, c0=c0: e.scalar_tensor_tensor(out=ztb[z2], in0=xt[sl], scalar=st1[:, c0 + 2:c0 + 3], in1=pregt, op0=ALU.mult, op1=ALU.mult),
              reads=[f"xt{sl}", ("s1", t, 2), "pregt"], writes=[f"zt{z2}"], regions=RG)

        def p1_b(t):
            z2 = t % 3
            pb = 2 * (t % 2)
            psb = ps[:, pb:pb + 2, :].rearrange("p a b -> p (a b)").bitcast(BF16)

            def trf(e, z2=z2, psb=psb):
                ins = None
                for kc in range(16):
                    ins = e.transpose(out=psb[:, kc * 128:(kc + 1) * 128], in_=ztb[z2][:, kc * 128:(kc + 1) * 128], identity=idb[:])
                return ins
            A("pe", trf, reads=[f"zt{z2}", "idb"], writes=[("ps", pb), ("ps", pb + 1)], regions=RG)
            zo = t * 128
            nn = 128 if t < 9 else 3
            A("act", lambda e, zo=zo, nn=nn, psb=psb: e.activation(out=zTm[:, 0:8, zo:zo + nn], in_=psb[:, 0:1024].rearrange("p (a b) -> p a b", a=8)[:, :, 0:nn], func=AF.Copy),
              reads=[("ps", pb)], writes=[("zT", t, 0)], regions=ZM)
            A("dve", lambda e, zo=zo, nn=nn, psb=psb: e.tensor_copy(out=zTm[:, 8:16, zo:zo + nn], in_=psb[:, 1024:2048].rearrange("p (a b) -> p a b", a=8)[:, :, 0:nn]),
              reads=[("ps", pb + 1)], writes=[("zT", t, 1)], regions=ZM)

        def v_banks(ti):
            return [2 * ti, 2 * ti + 1] if ti < 2 else [4 + 2 * (ti % 2), 5 + 2 * (ti % 2)]

        def v_pe(ti):
            tok = ti * 128
            bk = v_banks(ti)

            def vf(e, tok=tok, bk=bk):
                ins = None
                for kc in range(16):
                    for half in range(2):
                        ins = e.matmul(ps[:, bk[half], :], lhsT=zTm[:, kc, tok:tok + 128], rhs=Wv[:, kc, half * 512:(half + 1) * 512],
                                       start=(kc == 0), stop=(kc == 15))
                return ins
            A("pe", vf, reads=WVALL + zT_tokens(tok, tok + 128), writes=[("ps", b) for b in bk], regions=["ZP", "ZM"])

        def v_tile(ti, pe_done=False):
            s = ti % 2
            bk = v_banks(ti)
            if not pe_done:
                v_pe(ti)
            PSV = [("ps", b) for b in bk]
            vps = psflat(bk[0], 1024)
            A("act", lambda e, s=s, vps=vps: e.activation(out=vb[s], in_=vps, func=AF.Gelu_apprx_tanh), reads=PSV, writes=[("vb", s)] + PSV, regions=CS)
            c0 = 4 * ti
            A("act", lambda e, s=s, c0=c0: e.activation(out=vjunk, in_=vb[s], func=AF.Square, accum_out=st3[:, c0:c0 + 1]), reads=[("vb", s)], writes=["vjunk", ("s3", ti, 0)], regions=CS)
            A("dve", lambda e, c0=c0: e.tensor_scalar(out=st3[:, c0 + 1:c0 + 2], in0=st3[:, c0:c0 + 1], scalar1=1.0 / 1024, scalar2=EPS, op0=ALU.mult, op1=ALU.add),
              reads=[("s3", ti, 0)], writes=[("s3", ti, 1)])
            A("pool", lambda e, c0=c0: e.tensor_tensor(out=st3[:, c0 + 2:c0 + 3], in0=st3[:, c0 + 1:c0 + 2], in1=mhalf[:], op=ALU.pow), reads=[("s3", ti, 1), "mhalf"], writes=[("s3", ti, 2)])
            A("dve", lambda e, c0=c0: e.tensor_scalar(out=st3[:, c0 + 3:c0 + 4], in0=st3[:, c0 + 2:c0 + 3], scalar1=1.0, scalar2=None, op0=ALU.mult), reads=[("s3", ti, 2)], writes=[("s3", ti, 3)])
            rs = st3[:, c0 + 3:c0 + 4]
            if ti < 8:
                A("dve", lambda e, s=s, ti=ti, rs=rs: e.scalar_tensor_tensor(out=vn[:, ti, :], in0=vb[s], scalar=rs, in1=sgug, op0=ALU.mult, op1=ALU.mult),
                  reads=[("vb", s), ("s3", ti, 3), "sgug"], writes=[("vn", ti)], regions=CS + ["VN"])
            else:
                A("dve", lambda e, s=s, rs=rs: e.scalar_tensor_tensor(out=vb[s], in0=vb[s], scalar=rs, in1=sgug, op0=ALU.mult, op1=ALU.mult),
                  reads=[("vb", s), ("s3", ti, 3), "sgug"], writes=[("vb", s)], regions=CS)
                A("sp", lambda e, s=s: e.dma_start(out=vs_d, in_=vb[s]), reads=[("vb", s)], dma="o_vs", regions=CS)
                A("act", lambda e, s=s, ti=ti: e.activation(out=vn[:, ti, :], in_=vb[s], func=AF.Copy), reads=[("vb", s)], writes=[("vn", ti)], regions=CS + ["VN"])

        def const_a():
            A("act", lambda e: e.dma_start(out=cpk[:], in_=cpk_d), writes=["cpk"], dma="c0")
            A("act", lambda e: e.dma_start(out=cw[:], in_=cw_d), writes=["cw"], dma="c1")
            A("act", lambda e: e.dma_start(out=cst[:], in_=cst_d), writes=["cst"], dma="c2")
            A("act", lambda e: e.dma_start(out=hst[:], in_=hst_d), writes=["hst"], dma="c3")
            A("dve", lambda e: e.memset(stt[:], 0.0), writes=["stt"])
            T0 = der[:, D_T0:D_T0 + 8]
            T1 = der[:, D_T1:D_T1 + 8]
            T2 = der[:, D_T2:D_T2 + 8]
            lam = cpk[:, C_LAM:C_LAM + 8]
            HS8 = der[:, D_HS8:D_HS8 + 8]
            QS8 = der[:, D_QS8:D_QS8 + 8]
            A("dve", lambda e: e.tensor_scalar(out=T0, in0=lam, scalar1=-1.0, scalar2=None, op0=ALU.mult), reads=["cpk"], writes=["T0"])
            A("dve", lambda e: e.tensor_tensor(out=T1, in0=T0, in1=lam, op=ALU.min), reads=["T0", "cpk"], writes=["T1"])
            A("act", lambda e: e.activation(out=T1, in_=T1, func=AF.Exp), reads=["T1"], writes=["T1"])
            A("dve", lambda e: e.tensor_scalar(out=T2, in0=T1, scalar1=1.0, scalar2=None, op0=ALU.add), reads=["T1"], writes=["T2"])
            A("act", lambda e: e.activation(out=HS8, in_=T2, func=AF.Ln), reads=["T2"], writes=["HS8"])
            A("dve", lambda e: e.tensor_scalar(out=T2, in0=T2, scalar1=-1.0, scalar2=1e-30, op0=ALU.add, op1=ALU.max), reads=["T2", "HS8"], writes=["T2"])
            A("dve", lambda e: e.reciprocal(out=T2, in_=T2), reads=["T2"], writes=["T2"])
            A("dve", lambda e: e.tensor_tensor(out=T1, in0=T1, in1=T2, op=ALU.mult), reads=["T1", "T2"], writes=["T1"])
            A("dve", lambda e: e.tensor_tensor(out=T1, in0=T1, in1=HS8, op=ALU.mult), reads=["T1", "HS8"], writes=["T1"])
            A("dve", lambda e: e.tensor_scalar(out=T0, in0=T0, scalar1=0.0, scalar2=None, op0=ALU.max), reads=["T0"], writes=["T0"])
            A("dve", lambda e: e.tensor_tensor(out=T1, in0=T1, in1=T0, op=ALU.add), reads=["T1", "T0"], writes=["T1"])
            A("dve", lambda e: e.tensor_scalar(out=HS8, in0=T1, scalar1=-4.0, scalar2=None, op0=ALU.mult), reads=["T1"], writes=["HS8"])
            A("dve", lambda e: e.tensor_scalar(out=QS8, in0=T1, scalar1=2.0, scalar2=None, op0=ALU.mult), reads=["T1"], writes=["QS8"])
            A("dve", lambda e: e.tensor_scalar(out=der[:, D_HBR:D_HBR + 16], in0=cpk[:, C_BR:C_BR + 16], scalar1=0.5, scalar2=None, op0=ALU.mult),
              reads=["cpk"], writes=["HB"])
            A("dve", lambda e: e.tensor_scalar(out=der[:, D_FLA:D_FLA + 2], in0=cpk[:, C_FLAG:C_FLAG + 2], scalar1=0.5, scalar2=None, op0=ALU.mult),
              reads=["cpk"], writes=["FL"])
            return ["HS8", "QS8", "HB", "FL", "cpk", "cw"]

        def const_b():
            catl = cat[:, 0:8, :].rearrange("p a b -> p (a b)").bitcast(F32)
            ZP_ = ["ZP"]
            s_wr = zpf32[:, 0:1024].rearrange("p (a b) -> p a b", a=8)
            s_wi = zpf32[:, 1024:2048].rearrange("p (a b) -> p a b", a=8)
            s_wsT = zpf32[:, 2048:3072].rearrange("p (a b) -> p a b", a=8)
            s_wsS = zpf32[:, 3072:4096].rearrange("p (a b) -> p a b", a=8)
            s_mask = zpf32[:, 4096:4224]
            s_maskS = zpf32[:, 7296:7424]
            s_b = zpf32[0:1, 4224:6272]
            s_h = zpf32[0:1, 6272:7296].bitcast(BF16)
            s_l = zpf32[0:1, 0:1024].bitcast(BF16)
            A("sp", lambda e: e.dma_start(out=s_wr, in_=wr_d), writes=["s_wr"], dma="c4", regions=ZP_)
            A("sp", lambda e: e.dma_start(out=s_wi, in_=wi_d), writes=["s_wi"], dma="c5", regions=ZP_)
            A("sp", lambda e: e.dma_start(out=s_wsT, in_=wsT_d), writes=["s_wsT"], dma="c6", regions=ZP_)
            A("sp", lambda e: e.dma_start(out=s_wsS, in_=wsS_d), writes=["s_wsS"], dma="c7", regions=ZP_)
            A("sp", lambda e: e.dma_start(out=s_mask, in_=mask_d), writes=["s_mask"], dma="c8", regions=ZP_)
            A("sp", lambda e: e.dma_start(out=s_maskS, in_=maskS_d), writes=["s_maskS"], dma="c8s", regions=ZP_)
            A("sp", lambda e: e.dma_start(out=s_b, in_=bsp_d), writes=["s_b"], dma="c10", regions=ZP_)

            def p0():
                A("act", lambda e: e.activation(out=wr_b[:], in_=s_wr, func=AF.Copy), reads=["s_wr"], writes=["wr_b"], regions=ZP_)
                A("act", lambda e: e.activation(out=wi_b[:], in_=s_wi, func=AF.Copy), reads=["s_wi"], writes=["wi_b"], regions=ZP_)

            def p1():
                A("dve", lambda e: e.memset(biasK[:], 0.0), writes=["biasK"])
                A("dve", lambda e: e.memset(onesK[:], 0.0), writes=["onesK"])
                A("dve", lambda e: e.memset(onesK[0:2, :], 1.0), writes=["onesK"])

            def p1b():
                A("act", lambda e: e.activation(out=s_h, in_=s_b, func=AF.Copy), reads=["s_b"], writes=["s_h"], regions=ZP_)

            def p2():
                A("dve", lambda e: e.tensor_tensor(out=s_b, in0=s_b, in1=s_h, op=ALU.subtract), reads=["s_b", "s_h"], writes=["s_b"], regions=ZP_)

            def p2b():
                A("act", lambda e: e.activation(out=s_l, in_=s_b, func=AF.Copy), reads=["s_b"], writes=["s_l", "s_wr"], regions=ZP_)
                A("sp", lambda e: e.dma_start(out=biasK[0:1, :], in_=s_h), reads=["s_h", "biasK"], writes=["biasK", "constCL"], dma="c14", regions=ZP_)
                A("sp", lambda e: e.dma_start(out=biasK[1:2, :], in_=s_l), reads=["s_l", "biasK"], writes=["biasK", "constCL"], dma="c15", regions=ZP_)

            def p3():
                for hh in range(4):
                    A("dve", lambda e, hh=hh: e.tensor_tensor(out=wsT_b[:, hh, :], in0=s_wsT[:, hh, :], in1=s_mask, op=ALU.mult),
                      reads=["s_wsT", "s_mask"], writes=[("wsT_b", hh)], regions=ZP_)

            def p3b():
                for hh in range(4, 8):
                    A("dve", lambda e, hh=hh: e.tensor_tensor(out=wsT_b[:, hh, :], in0=s_wsT[:, hh, :], in1=s_mask, op=ALU.mult),
                      reads=["s_wsT", "s_mask"], writes=[("wsT_b", hh)], regions=ZP_)
                for hh in range(8):
                    A("dve", lambda e, hh=hh: e.tensor_tensor(out=wsS_b[:, hh, :], in0=s_wsS[:, hh, :], in1=s_maskS, op=ALU.mult),
                      reads=["s_wsS", "s_maskS"], writes=[("wsS_b", hh)] + (["constB"] if hh == 7 else []), regions=ZP_)
            return [p0, p3, p3b, p1, p1b, p2, p2b]


        CONST = const_a()
        for t in range(12):
            if t < 10:
                p1_a(t)
            if 0 <= t - 2 < 10:
                p1_b(t - 2)
            if t < 10:
                p1_a2(t)
            if t in (6, 9):
                for k, chunk in enumerate((0, 1, 8)):
                    if (k == 0) != (t == 6):
                        continue
                    dst = R3[:, 13312 + 1024 * k:13312 + 1024 * (k + 1)].bitcast(BF16).rearrange("p (a b) -> p a b", a=16)
                    A("pool", lambda e, chunk=chunk, dst=dst: e.dma_start(out=dst, in_=w40_d[chunk]), writes=[("eslab", k)], dma="sl_eslab_%d" % k)
        RG = ["R3"]

        slab_ctr = [0]

        SET2 = [("a", 2), ("tq", 2), ("m2", 2), ("xc", 2)]

        def slab_view(k):
            if k < 3:
                return R3[:, 13312 + 1024 * k:13312 + 1024 * (k + 1)].bitcast(BF16), [], ("eslab", k), []
            if k < 16:
                sl = k % 3
                return vnf[:, sl * 2048:(sl + 1) * 2048], ["VN"], ("slab", sl), []
            if k < 19:
                j = k - 16
                return R3[:, 6912 + 1024 * j:6912 + 1024 * (j + 1)].bitcast(BF16), ["R3"], ("pslab", j), SET2
            sl = k % 3
            return slab4[:, sl * 2048:(sl + 1) * 2048], ["ZP"], ("slab4", sl), []

        def load_slab_k(k, chunk):
            view, reg, tok, extra = slab_view(k)
            dst = view.rearrange("p (a b) -> p a b", a=16)
            A("pool", lambda e, chunk=chunk, dst=dst: e.dma_start(out=dst, in_=w40_d[chunk]), writes=[tok] + extra, dma="sl_%s_%s" % tok, regions=reg)

        def fjob(k, blocks, banks, split=False):
            view, reg, tok, _ = slab_view(k)
            groups = [[(bl, b)] for bl, b in zip(blocks, banks)] if split else [list(zip(blocks, banks))]
            for grp in groups:
                def f(e, grp=grp):
                    ins = None
                    for kc in range(16):
                        for (lo, n), b in grp:
                            ins = e.matmul(ps[:, b, 0:n], lhsT=view[:, kc * 128:(kc + 1) * 128], rhs=zTm[:, kc, lo:lo + n],
                                           start=(kc == 0), stop=(kc == 15))
                    return ins
                rd = [tok]
                for (lo, n), b in grp:
                    rd += zT_tokens(lo, lo + n)
                A("pe", f, reads=rd, writes=[("ps", b) for _, b in grp], regions=list(reg) + ZM)

        slab_seq = [0]
        for h in range(1, 8):
            slab_seq += [h, 8 + h - 1]
        slab_seq += [15]
        for h in range(8):
            slab_seq += [16 + h, 32 + h]
        sq_loaded = [3]
        sq_taken = [0]
        after_load = [None]

        def take_slab(chunk):
            k = sq_taken[0]
            assert slab_seq[k] == chunk, (k, chunk, slab_seq[k])
            while sq_loaded[0] < len(slab_seq) and sq_loaded[0] <= k + 2:
                load_slab_k(sq_loaded[0], slab_seq[sq_loaded[0]])
                sq_loaded[0] += 1
                if after_load[0] is not None:
                    after_load[0]()
            sq_taken[0] += 1
            return k

        BLK_M = [(0, 512), (512, 512), (1024, 128)]
        BLK_X = [(0, 512), (512, 512), (1024, 131)]

        S.new_epoch("R3")
        S.new_epoch("CL")
        S.new_epoch("CS")
        ZP = ["ZP"]
        slab4 = zpf[:, 9216:9216 + 6144]
        BIASC = ["biasK", "onesK"]
        constb_pieces = const_b()
        NS3 = 3
        abuf = [R3[:, s * 3456:s * 3456 + 1152] for s in range(NS3)]
        tqb = [R3[:, s * 3456 + 1152:s * 3456 + 2304] for s in range(NS3)]
        m2b = [R3[:, s * 3456 + 2304:s * 3456 + 3456] for s in range(NS3)]
        xcbb = [R3[:, 10368 + s * 576:10368 + (s + 1) * 576].bitcast(BF16) for s in range(NS3)]
        Qb = R3[:, 12096:12096 + 4096].bitcast(BF16).rearrange("p (a b) -> p a b", a=8)
        thbb = [catf[:, 0:1152], catf[:, 1152:2304]]
        xr = catf[:, 2304:2304 + 1203]
        xr_s = xr[:, 1027:1203].rearrange("p (s r) -> p s r", r=11)

        def x_pe(h):
            fjob(take_slab(h), BLK_X, [0, 1, 2], split=(h == 0))

        def x_evac(h):
            s = h % NS3
            ab = abuf[s]
            w3 = cw[:, h, 3:4]
            cb = cpk[:, C_CB + h:C_CB + h + 1]
            XC = [("xc", s)]
            PX = [("ps", 0), ("ps", 1), ("ps", 2)]
            A("act", lambda e, h=h: e.activation(out=xr_s[:, :, 0:3], in_=cst[:, h, :, :], func=AF.Copy), reads=["cst"], writes=["xr"], regions=CS)
            A("act", lambda e: e.activation(out=xr[:, 3:1027], in_=psflat(0, 1024), func=AF.Copy), reads=PX, writes=["xr"], regions=CS)
            A("act", lambda e: e.activation(out=xr_s[:, :, 3:11], in_=ps[:, 2, 0:128].rearrange("p (s t) -> p s t", t=8), func=AF.Copy),
              reads=PX, writes=["xr"], regions=CS)
            A("act", lambda e: e.activation(out=xr[:, 0:3], in_=ps[:, 2, 128:131], func=AF.Copy), reads=PX, writes=["xr"], regions=CS)
            A("act", lambda e, ab=ab, w3=w3, cb=cb: e.activation(out=ab[:, 0:NM], in_=psflat(0, NM), func=AF.Identity, scale=w3, bias=cb),
              reads=PX + CONST, writes=XC + [("a", s)], regions=RG)

        def x_taps(h):
            s = h % NS3
            ab = abuf[s]
            XC = [("xc", s)]
            xc_s = ab[:, 1024:NM].rearrange("p (s t) -> p s t", t=8)
            for k in range(3):
                A("dve", lambda e, k=k, h=h, ab=ab: e.scalar_tensor_tensor(out=ab[:, 0:1024], in0=xr[:, k:k + 1024], scalar=cw[:, h, k:k + 1], in1=ab[:, 0:1024],
                                                                        op0=ALU.mult, op1=ALU.add), reads=["xr"] + XC + CONST, writes=XC, regions=RG + CS)
                A("dve", lambda e, k=k, h=h, xc_s=xc_s: e.scalar_tensor_tensor(out=xc_s, in0=xr_s[:, :, k:k + 8], scalar=cw[:, h, k:k + 1], in1=xc_s,
                                                                            op0=ALU.mult, op1=ALU.add), reads=["xr"] + XC + CONST, writes=XC, regions=RG + CS)

        def x_cast(h):
            s = h % NS3
            ab, xcb = abuf[s], xcbb[s]
            XC = [("xc", s)]
            A("act", lambda e, ab=ab, xcb=xcb: e.activation(out=xcb[:, 0:NM], in_=ab[:, 0:NM], func=AF.Copy), reads=XC, writes=[("xcb", s)], regions=RG)
            A("act", lambda e, h=h: e.activation(out=stt[:, h, 0:3], in_=xr[:, 1024:1027], func=AF.Copy), reads=["xr", "stt"], writes=[("stt", h)], regions=CS)
            A("act", lambda e, h=h: e.activation(out=stt[:, h, 4:52].rearrange("p (s r) -> p s r", r=3), in_=xr_s[:, :, 8:11], func=AF.Copy),
              reads=["xr", "stt"], writes=[("stt", h)], regions=CS)

        GBLK = [(0, 512), (512, 512), (1024, 128)]

        def g_gates(h):
            s = h % NS3
            tq, m2, xcb = tqb[s], m2b[s], xcbb[s]
            hbr = der[:, D_HBR + h:D_HBR + h + 1]
            hbi = der[:, D_HBI + h:D_HBI + h + 1]
            j = 0
            for (lo, n) in GBLK:
                for (wmat, wname, dst, dname, bias) in ((wr_b, "wr_b", tq, "tqp", hbr), (wi_b, "wi_b", m2, "m2p", hbi)):
                    bank = 3 + (j % 2)
                    j += 1
                    A("pe", lambda e, lo=lo, n=n, bank=bank, wmat=wmat, xcb=xcb, h=h: e.matmul(ps[:, bank, 0:n], lhsT=wmat[:, h, :], rhs=xcb[:, lo:lo + n], start=True, stop=True),
                      reads=[("xcb", s), wname], writes=[("ps", bank)], regions=RG)
                    A("act", lambda e, lo=lo, n=n, bank=bank, dst=dst, bias=bias: e.activation(out=dst[:, lo:lo + n], in_=ps[:, bank, 0:n], func=AF.Tanh, scale=0.5, bias=bias),
                      reads=[("ps", bank)] + CONST, writes=[(dname, s, lo)] + ([("tq", s) if dname == "tqp" else ("m2", s)] if lo == 0 else []), regions=RG)

        def g_m2(h):
            s = h % NS3
            ab, m2 = abuf[s], m2b[s]
            XC = [("xc", s)]
            M2P = [("m2p", s, lo) for lo, _ in GBLK]
            A("dve", lambda e, m2=m2, ab=ab: e.scalar_tensor_tensor(out=m2[:, 0:NM], in0=m2[:, 0:NM], scalar=1.0, in1=ab[:, 0:NM], op0=ALU.add, op1=ALU.mult),
              reads=M2P + XC, writes=[("m2", s)], regions=RG)

        def g_exp(h):
            s = h % NS3
            ab, tq = abuf[s], tqb[s]
            hs8 = der[:, D_HS8 + h:D_HS8 + h + 1]
            qs8 = der[:, D_QS8 + h:D_QS8 + h + 1]
            XC = [("xc", s)]
            TQP = [("tqp", s, lo) for lo, _ in GBLK]
            A("act", lambda e, ab=ab, tq=tq, hs8=hs8: e.activation(out=ab[:, 0:NM], in_=tq[:, 0:NM], func=AF.Exp, scale=hs8, bias=hs8),
              reads=TQP + [("m2", s)] + CONST, writes=XC + [("a", s)], regions=RG)
            A("act", lambda e, tq=tq, qs8=qs8: e.activation(out=tq[:, 0:NM], in_=tq[:, 0:NM], func=AF.Tanh, scale=qs8, bias=qs8),
              reads=TQP + CONST, writes=[("tq", s)] + TQP, regions=RG)

        def g_gr(h):
            t2 = h % 2
            thb = thbb[t2]
            bkg = [5, 6, 7]
            fjob(take_slab(8 + h), BLK_M, bkg)
            A("act", lambda e, thb=thb: e.activation(out=thb, in_=psflat(5, NM), func=AF.Silu),
              reads=[("ps", b) for b in bkg], writes=[("thb", t2)], regions=CS)

        def c_sqrt(h):
            s = h % NS3
            tq = tqb[s]
            A("act", lambda e, tq=tq: e.activation(out=tq[:, 0:NM], in_=tq[:, 0:NM], func=AF.Sqrt), reads=[("tq", s)], writes=[("tq", s)], regions=RG)

        def c_main(h):
            s = h % NS3
            ab, tq, m2 = abuf[s], tqb[s], m2b[s]
            TQ = [("tq", s)]
            M2 = [("m2", s)]
            AA = [("a", s)]
            fla = der[:, D_FLA:D_FLA + 1]
            flb = der[:, D_FLB:D_FLB + 1]
            A("dve", lambda e, ab=ab, tq=tq: e.scalar_tensor_tensor(out=tq[:, 0:NM], in0=ab[:, 0:NM], scalar=1.0, in1=tq[:, 0:NM], op0=ALU.add, op1=ALU.mult),
              reads=TQ + AA, writes=TQ, regions=RG)
            A("dve", lambda e, tq=tq: e.tensor_scalar(out=stat[:, 8:9], in0=tq[:, 0:1], scalar1=flb, scalar2=fla, op0=ALU.mult, op1=ALU.add),
              reads=TQ + CONST, writes=["st8"], regions=RG)
            A("dve", lambda e, tq=tq, m2=m2: e.scalar_tensor_tensor(out=tq[:, 0:NM], in0=tq[:, 0:NM], scalar=0.5, in1=m2[:, 0:NM], op0=ALU.mult, op1=ALU.mult),
              reads=TQ + M2 + ["st8"], writes=TQ, regions=RG)
            A("dve", lambda e, tq=tq, m2=m2: e.tensor_tensor(out=tq[:, 0:1], in0=stat[:, 8:9], in1=m2[:, 0:1], op=ALU.mult),
              reads=TQ + M2 + ["st8"], writes=TQ, regions=RG)
            A("dve", lambda e, ab=ab: e.tensor_scalar(out=ab[:, 0:1], in0=ab[:, 0:1], scalar1=cpk[:, C_FLAG + 1:C_FLAG + 2], scalar2=None, op0=ALU.mult),
              reads=AA + CONST + TQ, writes=AA, regions=RG)
            a_s0 = ab[:, 1024:NM].rearrange("p (s t) -> p s t", t=8)[:, :, 0]
            b_s0 = tq[:, 1024:NM].rearrange("p (s t) -> p s t", t=8)[:, :, 0]
            A("dve", lambda e, a_s0=a_s0, h=h: e.tensor_tensor(out=stat[:, 16:32], in0=a_s0, in1=hst[:, h, :], op=ALU.mult),
              reads=AA + ["hst"], writes=["st16"], regions=RG)
            A("dve", lambda e, b_s0=b_s0: e.tensor_tensor(out=b_s0, in0=b_s0, in1=stat[:, 16:32], op=ALU.add),
              reads=TQ + ["st16"], writes=TQ, regions=RG)
            A("dve", lambda e, a_s0=a_s0: e.memset(a_s0, 0.0), reads=["st16"], writes=AA, regions=RG)
            A("dve", lambda e, ab=ab, tq=tq, m2=m2: e.tensor_tensor_scan(out=m2[:, 0:NM], data0=ab[:, 0:NM], data1=tq[:, 0:NM], initial=0.0, op0=ALU.mult, op1=ALU.add),
              reads=AA + TQ + M2, writes=M2, regions=RG)

        def c_tail(h):
            s = h % NS3
            t2 = h % 2
            ab, tq, m2, thb = abuf[s], tqb[s], m2b[s], thbb[t2]
            TQ = [("tq", s)]
            M2 = [("m2", s)]
            AA = [("a", s)]
            A("dve", lambda e, ab=ab, tq=tq: e.tensor_tensor_scan(out=tq[:, 0:1024], data0=ab[:, 0:1024], data1=ab[:, 0:1024], initial=1.0, op0=ALU.mult, op1=ALU.bypass),
              reads=AA + TQ, writes=TQ, regions=RG)
            A("dve", lambda e, m2=m2, h=h, thb=thb: e.tensor_tensor(out=cat[:, h, :], in0=m2[:, 0:NM], in1=thb, op=ALU.mult),
              reads=M2 + [("thb", t2), "constCL"], writes=[("cat", h)], regions=RG + CS + CL)
            A("dve", lambda e, tq=tq, h=h, thb=thb: e.tensor_tensor(out=Qb[:, h, :], in0=tq[:, 0:1024], in1=thb[:, 0:1024], op=ALU.mult),
              reads=TQ + [("thb", t2)], writes=[("Q", h)], regions=RG + CS)
            A("act", lambda e, m2=m2, h=h: e.activation(out=hx[:, h:h + 1], in_=m2[:, 1023:1024], func=AF.Copy), reads=M2, writes=[("hx", h)], regions=RG)
            A("act", lambda e, tq=tq, h=h: e.activation(out=pend[:, h:h + 1], in_=tq[:, 1023:1024], func=AF.Copy), reads=TQ, writes=[("pend", h)], regions=RG)
            A("act", lambda e, m2=m2, h=h: e.activation(out=stt[:, h, 3:4], in_=m2[:, 1023:1024], func=AF.Copy), reads=M2 + ["stt"], writes=[("stt", h)], regions=RG)
            A("act", lambda e, m2=m2, h=h: e.activation(out=stt[:, h, 52:68], in_=m2[:, 1024:NM].rearrange("p (s t) -> p s t", t=8)[:, :, 7], func=AF.Copy),
              reads=M2 + ["stt"], writes=[("stt", h)], regions=RG)

        for i in range(10):
            hx_, hg, hc = i, i - 1, i - 2
            if hx_ < 8:
                x_pe(hx_)
            if hx_ >= 8:
                v_pe(hx_ - 8)
            for j in (2 * i, 2 * i + 1):
                if j < len(constb_pieces):
                    constb_pieces[j]()
            if i == 4:
                S.new_epoch("ZP")
                load_wv()
            if 0 <= hg < 8:
                g_gates(hg)
            if 0 <= hc < 8:
                c_main(hc)
            if 0 <= hg < 8:
                g_m2(hg)
            if hx_ < 8:
                x_evac(hx_)
            if 0 <= hg < 8:
                g_exp(hg)
                c_sqrt(hg)
            if 0 <= hc < 8:
                c_tail(hc)
            if hx_ < 8:
                x_taps(hx_)
            if 0 <= hg < 8:
                g_gr(hg)
            if hx_ < 8:
                x_cast(hx_)

        def start_exchange():
            HXALL = [("hx", h) for h in range(8)]
            A("pool", lambda e: e.dma_start(out=hx_in.ap(), in_=hx[:]), reads=HXALL, writes=["hx_in"], dma="hxo")
            A("pool", lambda e: e.collective_compute("AllGather", ALU.bypass, replica_groups=[[0, 1], [2, 3], [4, 5], [6, 7]],
                                                     ins=[hx_in.ap().opt()], outs=[hx_out.ap().opt()]),
              reads=["hx_in"], writes=["hx_out"], dma="cc", inc=1)


        QALL = [("Q", h) for h in range(8)]

        def fix_begin():
            A("sp", lambda e: e.dma_start(out=hin[:], in_=hx_out.ap()[0:128, :]), reads=["hx_out"], writes=["hin"], dma="hxi")

        def fix_heads(hs):
            for h in hs:
                A("dve", lambda e, h=h: e.scalar_tensor_tensor(out=cat[:, h, 0:1024], in0=Qb[:, h, :], scalar=hin[:, h:h + 1], in1=cat[:, h, 0:1024], op0=ALU.mult, op1=ALU.add),
                  reads=[("Q", h), "hin", ("cat", h)], writes=[("cat", h)], regions=RG)

        def fix_end():
            stt_h = stt[:, :, 3]
            PENDALL = [("pend", h) for h in range(8)]
            A("dve", lambda e: e.tensor_tensor(out=pend[:], in0=pend[:], in1=hin[:], op=ALU.mult), reads=PENDALL + ["hin"], writes=PENDALL)
            A("dve", lambda e: e.tensor_tensor(out=stt_h, in0=stt_h, in1=pend[:], op=ALU.add), reads=PENDALL + [("stt", h) for h in range(8)] + ["stt"], writes=["stt"])
            A("sp", lambda e: e.dma_start(out=st_d, in_=stt[:]), reads=[("stt", h) for h in range(8)] + ["stt"], dma="o_st")
        Wo = R3[:, :].bitcast(BF16).rearrange("p (a b) -> p a b", a=16)
        WOALL = [("Wo", q) for q in range(16)]
        WO_ORDER = [0, 1, 2, 3, 4, 5, 10, 6, 7, 8, 9, 11, 12, 13, 14, 15]
        wo_next = [0]
        wo_limit = [0]

        def load_wo_piece():
            i = wo_next[0]
            if i < min(16, wo_limit[0]):
                wo_next[0] += 1
                q = WO_ORDER[i]
                extra = []
                if 6 <= q <= 9:
                    extra = [("pslab", 0), ("pslab", 1), ("pslab", 2)]
                if q >= 11:
                    extra = list(QALL)
                A("pool", lambda e, q=q: e.dma_start(out=Wo[:, q:q + 1, :], in_=wout_d[:, q:q + 1, :]), writes=[("Wo", q)] + extra, dma="wout", regions=RG)

        S.new_epoch("CS")
        S.new_epoch("VN")
        A("sp", lambda e: e.dma_start(out=sgug, in_=sgug_d), writes=["sgug"], dma="c12", regions=CS)
        S.new_epoch("R3")
        wo_limit[0] = 4
        for ti in range(9):
            v_tile(ti, pe_done=(ti < 2))
            if ti % 2 == 1:
                load_wo_piece()
        wo_limit[0] = 7

        def wo_hook():
            load_wo_piece()
            load_wo_piece()
        after_load[0] = wo_hook

        S.new_epoch("ZP")
        S.new_epoch("CS")
        ub = [zpf32[:, 0:1152], zpf32[:, 1152:2304]]
        gb = [zpf32[:, 2304:3456], zpf32[:, 3456:4608]]
        def sgu_u_pe(h):
            fjob(take_slab(16 + h), BLK_M, [0, 1, 2])

        def sgu_u_chain(h):
            s = h % 2
            PSU = [("ps", b) for b in (0, 1, 2)]
            ups = psflat(0, NM)
            A("act", lambda e, s=s, ups=ups: e.activation(out=ub[s], in_=ups, func=AF.Gelu_apprx_tanh), reads=PSU + ["constB"], writes=[("ub", s)] + PSU, regions=ZP)

        def sgu_gs(h):
            s = h % 2
            bkg = [3, 4, 5]
            fjob(take_slab(32 + h), BLK_M, bkg)
            PSG = [("ps", b) for b in bkg]
            gps = psflat(3, NM)
            A("act", lambda e, s=s, gps=gps: e.activation(out=gb[s], in_=gps, func=AF.Silu), reads=PSG + ["constB"], writes=[("gb", s)] + PSG, regions=ZP)
            A("dve", lambda e, s=s: e.tensor_tensor(out=ub[s], in0=ub[s], in1=gb[s], op=ALU.mult), reads=[("ub", s), ("gb", s)], writes=[("ub", s)], regions=ZP)

        def sgu_spatial(h):
            s = h % 2
            for g0, cs in ((0, [0, 1, 2, 3]), (1, [4, 5, 6, 7]), (0, [8])):
                bank = 6 + g0

                def spf(e, cs=cs, bank=bank, h=h):
                    ins = None
                    for j, c in enumerate(cs):
                        o = ps[:, bank, j * 128:(j + 1) * 128]
                        wmat = wsT_b if c < 8 else wsS_b
                        boff = h * 128 if c < 8 else 1024 + h * 128
                        e.matmul(o, lhsT=vn[:, c, h * 128:(h + 1) * 128], rhs=wmat[:, h, :], start=True, stop=False)
                        ins = e.matmul(o, lhsT=onesK[:, :], rhs=biasK[:, boff:boff + 128], start=False, stop=True)
                    return ins
                A("pe", spf, reads=[("vn", c) for c in cs] + [("wsT_b", hh) for hh in range(8)] + [("wsS_b", hh) for hh in range(8)] + BIASC, writes=[("ps", bank)], regions=["VN"])
                n = 128 * len(cs)
                t0 = cs[0] * 128
                A("dve", lambda e, bank=bank, n=n, t0=t0, h=h, s=s: e.scalar_tensor_tensor(out=cat[:, 8 + h, t0:t0 + n], in0=ps[:, bank, 0:n], scalar=1.0,
                                                                                      in1=ub[s][:, t0:t0 + n], op0=ALU.mult, op1=ALU.mult),
                  reads=[("ps", bank), ("ub", s)], writes=[("cat", 8 + h, t0), ("ps", bank)], regions=ZP + CS)

        sgu_u_pe(0)
        sgu_u_chain(0)
        sgu_gs(0)
        start_exchange()
        for h in range(8):
            if h + 1 < 8:
                sgu_u_pe(h + 1)
            sgu_spatial(h)
            if h + 1 < 8:
                sgu_u_chain(h + 1)
                sgu_gs(h + 1)
            if h == 1:
                fix_begin()
            if h == 2:
                wo_limit[0] = 11
            if 1 <= h <= 4:
                fix_heads([2 * (h - 1), 2 * (h - 1) + 1])
            if h == 4:
                fix_end()
            if h == 5:
                wo_limit[0] = 16

        wo_limit[0] = 16
        while wo_next[0] < 16:
            load_wo_piece()
        S.new_epoch("ZP")
        S.new_epoch("ZM")
        xo = [zmf32[:, 0:2048], zmf32[:, 2048:4096]]
        yo = [zmf32[:, 4096:6144], zmf32[:, 6144:8192]]
        postg = zpf32[:, 0:2048]
        ojunk = zpf32[:, 2048:3072].bitcast(BF16)
        A("sp", lambda e: e.dma_start(out=postg, in_=postg_d), writes=["postg"], dma="c13", regions=ZP)
        CATALL = [("cat", k) for k in range(8)] + [("cat", 8 + k, t0) for k in range(8) for t0 in (0, 512, 1024)]
        for ti in range(9):
            s = ti % 2
            tok = ti * 128
            A("sp", lambda e, s=s, ti=ti: e.dma_start(out=xo[s], in_=x_d[ti * 128:(ti + 1) * 128, :]), writes=[("xo", s)], dma=f"xo{s}", regions=ZM)
            bk = [4 * s + q for q in range(4)]

            def of(e, tok=tok, bk=bk):
                ins = None
                for kc in range(16):
                    for q in range(4):
                        ins = e.matmul(ps[:, bk[q], :], lhsT=cat[:, kc, tok:tok + 128], rhs=Wo[:, kc, q * 512:(q + 1) * 512], start=(kc == 0), stop=(kc == 15))
                return ins
            A("pe", of, reads=WOALL + CATALL, writes=[("ps", b) for b in bk], regions=RG)
            PSO = [("ps", b) for b in bk]
            ops_ = psflat(bk[0], 2048)
            A("act", lambda e, ops_=ops_, ti=ti: e.activation(out=ojunk, in_=ops_, func=AF.Square, accum_out=st5[:, 3 * ti:3 * ti + 1]), reads=PSO, writes=["ojunk", ("s5", ti, 0)], regions=ZP)
            A("dve", lambda e, ti=ti: e.tensor_scalar(out=st5[:, 3 * ti + 1:3 * ti + 2], in0=st5[:, 3 * ti:3 * ti + 1], scalar1=1.0 / D, scalar2=EPS, op0=ALU.mult, op1=ALU.add),
              reads=[("s5", ti, 0)], writes=[("s5", ti, 1)])
            A("pool", lambda e, ti=ti: e.tensor_tensor(out=st5[:, 3 * ti + 2:3 * ti + 3], in0=st5[:, 3 * ti + 1:3 * ti + 2], in1=mhalf[:], op=ALU.pow), reads=[("s5", ti, 1), "mhalf"], writes=[("s5", ti, 2)])
            A("dve", lambda e, s=s, ops_=ops_, ti=ti: e.scalar_tensor_tensor(out=yo[s], in0=ops_, scalar=st5[:, 3 * ti + 2:3 * ti + 3], in1=postg, op0=ALU.mult, op1=ALU.mult),
              reads=PSO + [("s5", ti, 2), "postg"], writes=[("yo", s)] + PSO, regions=ZP + ZM)
            if ti < 8:
                A("pool", lambda e, s=s: e.tensor_tensor(out=yo[s], in0=yo[s], in1=xo[s], op=ALU.add), reads=[("yo", s), ("xo", s)], writes=[("yo", s)], regions=ZM)
                A("sp", lambda e, s=s, ti=ti: e.dma_start(out=y_d[ti * 128:(ti + 1) * 128, :], in_=yo[s]), reads=[("yo", s)], dma=f"yo{s}", regions=ZM)
            else:
                for hf in range(2):
                    c0_, c1_ = hf * 1024, (hf + 1) * 1024
                    A("dve", lambda e, s=s, c0_=c0_, c1_=c1_: e.tensor_tensor(out=yo[s][:, c0_:c1_], in0=yo[s][:, c0_:c1_], in1=xo[s][:, c0_:c1_], op=ALU.add),
                      reads=[("yo", s), ("xo", s)], writes=[("yo", s, hf)], regions=ZM)
                    A("sp", lambda e, s=s, ti=ti, c0_=c0_, c1_=c1_: e.dma_start(out=y_d[ti * 128:(ti + 1) * 128, c0_:c1_], in_=yo[s][:, c0_:c1_]),
                      reads=[("yo", s, hf)], dma=f"yl{hf}", regions=ZM)
        A("sp", None, writes=[("yo", 0), ("yo", 1), ("yo", 0, 0), ("yo", 0, 1), ("vb", 0), ("vb", 1), "stt"] + [("stt", h) for h in range(8)])
        S.emit(nc, st)
    return nc


_NC_CACHE = {}


def _prep_inputs(x_prompt, x_sample, state_rglru_conv, state_rglru_h, pre_norm_g, post_norm_g,
                 w_in, conv_w, conv_b, w_rgate, b_rgate, w_igate, b_igate, lru_lambda,
                 sgu_norm_g, w_spatial, b_spatial, w_out):
    f = lambda a: np.ascontiguousarray(np.asarray(a, dtype=np.float32))
    x_prompt, x_sample = f(x_prompt), f(x_sample)
    w = f(w_in)[0]
    w40 = np.ascontiguousarray(w.reshape(16, 128, 40, 128).transpose(2, 1, 0, 3))
    wout = np.ascontiguousarray(f(w_out)[0].reshape(16, 128, D).transpose(1, 0, 2))
    wv = np.ascontiguousarray(w[:, 3072:4096].reshape(16, 128, 1024).transpose(1, 0, 2))
    fm = lambda v: np.ascontiguousarray(f(v).reshape(8, 128).T)
    cw = np.ascontiguousarray(f(conv_w)[0].reshape(4, 8, 128).transpose(2, 1, 0))
    wr = np.ascontiguousarray(f(w_rgate)[0].transpose(1, 0, 2))
    wi = np.ascontiguousarray(f(w_igate)[0].transpose(1, 0, 2))
    ws = f(w_spatial)[0]
    wsT = np.ascontiguousarray(ws.transpose(2, 0, 1))
    wsS = np.zeros((128, 8, 128), np.float32)
    maskS = np.zeros((128, 128), np.float32)
    tri8 = np.triu(np.ones((8, 8), np.float32))
    for n in range(16):
        wsS[n * 8:(n + 1) * 8, :, n * 8:(n + 1) * 8] = ws[:, :8, :8].transpose(2, 0, 1)
        maskS[n * 8:(n + 1) * 8, n * 8:(n + 1) * 8] = tri8
    mask = np.triu(np.ones((128, 128), np.float32))
    idn = np.eye(128, dtype=np.float32)
    bs = f(b_spatial)[0]
    bsp = np.concatenate([bs.reshape(-1), np.tile(bs[:, :8], (1, 16)).reshape(-1)])[None, :]
    sgug = np.ascontiguousarray(np.broadcast_to(f(sgu_norm_g)[0][None, :], (128, 1024)))
    postg = np.ascontiguousarray(np.broadcast_to(f(post_norm_g)[0][None, :], (128, D)))
    pregt = np.ascontiguousarray(np.broadcast_to(f(pre_norm_g)[0][None, :], (128, D)))
    sc = f(state_rglru_conv)[0]
    sh = f(state_rglru_h)[0]
    maps = []
    for c in range(8):
        b, half = c // 2, c % 2
        x = np.zeros((NXR, D), np.float32)
        x[0:1024] = x_prompt[b, half * 1024:(half + 1) * 1024]
        x[1024:1152] = x_sample[16 * c:16 * c + 16].reshape(128, D)
        if half == 1:
            x[1152:1155] = x_prompt[b, 1021:1024]
        cpk = np.zeros((128, 64), np.float32)
        cpk[:, 0:8] = fm(f(conv_b)[0])
        cpk[:, 8:16] = f(b_rgate)[0].T
        cpk[:, 16:24] = f(b_igate)[0].T
        cpk[:, 24:32] = fm(f(lru_lambda)[0])
        cpk[:, 32] = 1.0 if half == 0 else 0.0
        cpk[:, 33] = 0.0 if half == 0 else 1.0
        cst = np.ascontiguousarray(sc[16 * c:16 * c + 16].reshape(16, 3, 8, 128).transpose(3, 2, 0, 1))
        hst = np.ascontiguousarray(sh[16 * c:16 * c + 16].reshape(16, 8, 128).transpose(2, 1, 0))
        maps.append(dict(x=x, w40=w40, wout=wout, wv=wv, cpk=cpk, cw=cw, wr=wr, wi=wi, wsT=wsT, wsS=wsS, mask=mask, maskS=maskS, idn=idn,
                         bsp=np.ascontiguousarray(bsp), sgug=sgug, postg=postg, pregt=pregt, cst=cst, hst=hst))
    return maps


def kernel(**inputs):
    maps = _prep_inputs(**inputs)
    if "nc" not in _NC_CACHE:
        _NC_CACHE["nc"] = build_nc()
    nc = _NC_CACHE["nc"]
    res = run_bass_kernel_spmd(nc, maps, core_ids=list(range(8)))
    R = res.results
    y_prompt = np.zeros((4, 2048, D), np.float32)
    y_sample = np.zeros((128, 8, D), np.float32)
    conv_p = np.zeros((1, 4, 3, 1024), np.float32)
    h_p = np.zeros((1, 4, 1024), np.float32)
    conv_s = np.zeros((1, 128, 3, 1024), np.float32)
    h_s = np.zeros((1, 128, 1024), np.float32)
    v_s = np.zeros((1, 128, 8, 1024), np.float32)
    for c in range(8):
        b, half = c // 2, c % 2
        r = R[c]
        y = np.asarray(r["y"])
        y_prompt[b, half * 1024:(half + 1) * 1024] = y[0:1024]
        y_sample[16 * c:16 * c + 16] = y[1024:1152].reshape(16, 8, D)
        stt = np.asarray(r["st"])
        v_s[0, 16 * c:16 * c + 16] = np.asarray(r["vs"]).reshape(16, 8, 1024)
        cs = stt[:, :, 4:52].reshape(128, 8, 16, 3)
        conv_s[0, 16 * c:16 * c + 16] = cs.transpose(2, 3, 1, 0).reshape(16, 3, 1024)
        h_s[0, 16 * c:16 * c + 16] = stt[:, :, 52:68].transpose(2, 1, 0).reshape(16, 1024)
        if half == 1:
            conv_p[0, b] = stt[:, :, 0:3].transpose(2, 1, 0).reshape(3, 1024)
            h_p[0, b] = stt[:, :, 3].T.reshape(1024)
    return (y_prompt, y_sample, conv_p, h_p, conv_s, h_s, v_s)
```
